# Optimizing a Trainium2 kernel written in Bass

```python
import math
import jax, jax.numpy as jnp
from jax import lax
import numpy as np

D_MODEL = 1024
BATCH = 8
SEQ = 4096
DEPTH = 1

CHUNK = 64
D_MIX = D_MODEL
SSD_WIDTH = D_MIX // 2
SSD_HEADS = 8
SSD_HEAD_DIM = SSD_WIDTH // SSD_HEADS
SSD_GROUPS = 2
SSD_STATE = 128
SSD_CONV = 4
SSD_XBC = SSD_WIDTH + 2 * SSD_GROUPS * SSD_STATE
CONF_WIDTH = D_MIX - SSD_WIDTH
CONF_CONV = 31
IN_WIDTH = SSD_WIDTH + SSD_XBC + SSD_HEADS + 2 * CONF_WIDTH
PEER_HEADS = 8
PEER_N_KEYS = 128
PEER_N_EXPERTS = PEER_N_KEYS * PEER_N_KEYS
PEER_D_QUERY = 256
PEER_D_HALF = PEER_D_QUERY // 2
PEER_TOPK = 16
TOKEN_BLOCK = 128
NORM_EPS = 1e-6

kernel_name = 'hybrid_ssd_conformer_peer_block'


def rmsnorm(x, g):
    xf = x.astype(jnp.float32)
    y = xf * lax.rsqrt(jnp.mean(xf * xf, axis=-1, keepdims=True) + NORM_EPS)
    return (y * g.astype(jnp.float32)).astype(x.dtype)


def causal_dwconv(x, w, b):
    k = w.shape[0]
    y = lax.conv_general_dilated(x, w[:, None, :].astype(x.dtype), window_strides=(1,),
                                 padding=[(k - 1, 0)],
                                 dimension_numbers=('NWC', 'WIO', 'NWC'),
                                 feature_group_count=x.shape[-1])
    return y + b.astype(x.dtype)


def ssd_chunked(x, dt, a, bm, cm):
    bsz, s, nh, p = x.shape
    nc = s // CHUNK
    x = x.reshape(bsz, nc, CHUNK, nh, p)
    bm = bm.reshape(bsz, nc, CHUNK, nh, -1)
    cm = cm.reshape(bsz, nc, CHUNK, nh, -1)
    dt = dt.reshape(bsz, nc, CHUNK, nh)
    a_cs = jnp.cumsum(dt * a, axis=2)
    mask = jnp.tril(jnp.ones((CHUNK, CHUNK), dtype=bool))[None, None, :, :, None]
    seg = a_cs[:, :, :, None, :] - a_cs[:, :, None, :, :]
    decay = jnp.exp(jnp.where(mask, seg, -jnp.inf))
    scores = jnp.einsum('bcthn,bcshn->bctsh', cm, bm) * decay * dt[:, :, None, :, :]
    y_diag = jnp.einsum('bctsh,bcshp->bcthp', scores, x)
    to_end = jnp.exp(a_cs[:, :, -1:, :] - a_cs) * dt
    states = jnp.einsum('bcshn,bcsh,bcshp->bchpn', bm, to_end, x)
    chunk_decay = jnp.exp(a_cs[:, :, -1, :])

    def step(h, inp):
        dec, st = inp
        return dec[:, :, None, None] * h + st, h

    h0 = jnp.zeros((bsz, nh, p, bm.shape[-1]), jnp.float32)
    _, h_in = lax.scan(step, h0, (jnp.moveaxis(chunk_decay, 1, 0), jnp.moveaxis(states, 1, 0)))
    h_in = jnp.moveaxis(h_in, 0, 1)
    y_off = jnp.einsum('bcthn,bchpn->bcthp', cm, h_in) * jnp.exp(a_cs)[..., None]
    return (y_diag + y_off).reshape(bsz, s, nh, p)


def ssd_branch(z, xbc, dt_raw, conv_w, conv_b, dt_bias, a_log, d_skip, norm_g):
    bsz, s, _ = z.shape
    f32 = jnp.float32
    xbc = jax.nn.silu(causal_dwconv(xbc, conv_w, conv_b))
    xs, b_in, c_in = jnp.split(xbc, [SSD_WIDTH, SSD_WIDTH + SSD_GROUPS * SSD_STATE], axis=-1)
    rep = SSD_HEADS // SSD_GROUPS
    xh = xs.reshape(bsz, s, SSD_HEADS, SSD_HEAD_DIM).astype(f32)
    bh = jnp.repeat(b_in.reshape(bsz, s, SSD_GROUPS, SSD_STATE), rep, axis=2).astype(f32)
    ch = jnp.repeat(c_in.reshape(bsz, s, SSD_GROUPS, SSD_STATE), rep, axis=2).astype(f32)
    dt = jax.nn.softplus(dt_raw.astype(f32) + dt_bias.astype(f32))
    a = -jnp.exp(a_log.astype(f32))
    y = ssd_chunked(xh, dt, a, bh, ch) + d_skip.astype(f32)[:, None] * xh
    y = y.reshape(bsz, s, SSD_WIDTH) * jax.nn.silu(z.astype(f32))
    return rmsnorm(y, norm_g).astype(z.dtype)


def conformer_conv_branch(glu_in, dw_w, dw_b, ln_g, ln_b):
    val, gate = jnp.split(glu_in, 2, axis=-1)
    u = causal_dwconv(val * jax.nn.sigmoid(gate), dw_w, dw_b)
    uf = u.astype(jnp.float32)
    mu = jnp.mean(uf, axis=-1, keepdims=True)
    var = jnp.mean(jnp.square(uf - mu), axis=-1, keepdims=True)
    un = (uf - mu) * lax.rsqrt(var + NORM_EPS) * ln_g.astype(jnp.float32) + ln_b.astype(jnp.float32)
    return jax.nn.silu(un).astype(u.dtype)


def peer_ffn(xn, w_query, sub_keys, expert_u, expert_v):
    bsz, s, d = xn.shape
    xt = xn.reshape(bsz * s // TOKEN_BLOCK, TOKEN_BLOCK, d)

    def block(xb):
        tb = xb.shape[0]
        q = (xb @ w_query).reshape(tb, PEER_HEADS, 2, PEER_D_HALF)
        sc = jnp.einsum('thid,hikd->thik', q, sub_keys).astype(jnp.float32)
        sv, si = lax.top_k(sc, PEER_TOPK)
        cand = (sv[:, :, 0, :, None] + sv[:, :, 1, None, :]).reshape(tb, PEER_HEADS, -1)
        cand_idx = (si[:, :, 0, :, None] * PEER_N_KEYS + si[:, :, 1, None, :]).reshape(tb, PEER_HEADS, -1)
        best, pos = lax.top_k(cand, PEER_TOPK)
        idx = jnp.take_along_axis(cand_idx, pos, axis=-1)
        g = jax.nn.softmax(best, axis=-1)
        u = expert_u[idx]
        v = expert_v[idx]
        act = jax.nn.gelu(jnp.einsum('td,thkd->thk', xb, u).astype(jnp.float32), approximate=False)
        return jnp.einsum('thk,thkd->td', (g * act).astype(xb.dtype), v)

    return lax.map(block, xt).reshape(bsz, s, d)


def setup_inputs(seed: int = 0) -> dict:
    key = jax.random.key(seed)
    ks = jax.random.split(key, 24)
    f32 = jnp.float32

    def nrm(k, shape, scale):
        return jax.random.normal(k, shape, f32) * scale

    dt0 = jnp.exp(jax.random.uniform(ks[8], (DEPTH, SSD_HEADS), f32) * (math.log(0.1) - math.log(0.001)) + math.log(0.001))
    return {
        'x': nrm(ks[0], (BATCH, SEQ, D_MODEL), 1.0),
        'c': nrm(ks[1], (BATCH, D_MODEL), 1.0),
        'ada_w': nrm(ks[2], (DEPTH, D_MODEL, 6 * D_MODEL), 0.5 * D_MODEL ** -0.5),
        'ada_b': nrm(ks[3], (DEPTH, 6 * D_MODEL), 0.02),
        'norm1_g': 1.0 + nrm(ks[4], (DEPTH, D_MODEL), 0.02),
        'w_in': nrm(ks[5], (DEPTH, D_MODEL, IN_WIDTH), D_MODEL ** -0.5),
        'ssd_conv_w': nrm(ks[6], (DEPTH, SSD_CONV, SSD_XBC), SSD_CONV ** -0.5),
        'ssd_conv_b': nrm(ks[7], (DEPTH, SSD_XBC), 0.02),
        'ssd_dt_bias': dt0 + jnp.log(-jnp.expm1(-dt0)),
        'ssd_a_log': jnp.log(jax.random.uniform(ks[9], (DEPTH, SSD_HEADS), f32, minval=1.0, maxval=16.0)),
        'ssd_d': 1.0 + nrm(ks[10], (DEPTH, SSD_HEADS), 0.02),
        'ssd_norm_g': 1.0 + nrm(ks[11], (DEPTH, SSD_WIDTH), 0.02),
        'conf_dw_w': nrm(ks[12], (DEPTH, CONF_CONV, CONF_WIDTH), CONF_CONV ** -0.5),
        'conf_dw_b': nrm(ks[13], (DEPTH, CONF_WIDTH), 0.02),
        'conf_ln_g': 1.0 + nrm(ks[14], (DEPTH, CONF_WIDTH), 0.02),
        'conf_ln_b': nrm(ks[15], (DEPTH, CONF_WIDTH), 0.02),
        'w_out': nrm(ks[16], (DEPTH, D_MIX, D_MODEL), D_MIX ** -0.5),
        'norm2_g': 1.0 + nrm(ks[17], (DEPTH, D_MODEL), 0.02),
        'peer_w_query': nrm(ks[18], (DEPTH, D_MODEL, PEER_HEADS * PEER_D_QUERY), D_MODEL ** -0.5),
        'peer_sub_keys': nrm(ks[19], (DEPTH, PEER_HEADS, 2, PEER_N_KEYS, PEER_D_HALF), PEER_D_HALF ** -0.5),
        'peer_u': nrm(ks[20], (DEPTH, PEER_N_EXPERTS, D_MODEL), D_MODEL ** -0.5),
        'peer_v': nrm(ks[21], (DEPTH, PEER_N_EXPERTS, D_MODEL), D_MODEL ** -0.5),
        'final_norm_g': 1.0 + nrm(ks[22], (D_MODEL,), 0.02),
    }


def reference(x, c, ada_w, ada_b, norm1_g, w_in, ssd_conv_w, ssd_conv_b, ssd_dt_bias, ssd_a_log,
              ssd_d, ssd_norm_g, conf_dw_w, conf_dw_b, conf_ln_g, conf_ln_b, w_out, norm2_g,
              peer_w_query, peer_sub_keys, peer_u, peer_v, final_norm_g):
    h = x
    cond = jax.nn.silu(c)
    for l in range(DEPTH):
        mod = cond @ ada_w[l] + ada_b[l]
        sh1, sc1, g1, sh2, sc2, g2 = [m[:, None, :] for m in jnp.split(mod, 6, axis=-1)]
        hn = rmsnorm(h, norm1_g[l]) * (1.0 + sc1) + sh1
        proj = hn @ w_in[l]
        z, xbc, dt_raw, glu_in = jnp.split(
            proj, [SSD_WIDTH, SSD_WIDTH + SSD_XBC, SSD_WIDTH + SSD_XBC + SSD_HEADS], axis=-1)
        y_ssd = ssd_branch(z, xbc, dt_raw, ssd_conv_w[l], ssd_conv_b[l], ssd_dt_bias[l],
                           ssd_a_log[l], ssd_d[l], ssd_norm_g[l])
        y_conf = conformer_conv_branch(glu_in, conf_dw_w[l], conf_dw_b[l],
                                       conf_ln_g[l], conf_ln_b[l])
        h = h + g1 * (jnp.concatenate([y_ssd, y_conf], axis=-1) @ w_out[l])
        hn = rmsnorm(h, norm2_g[l]) * (1.0 + sc2) + sh2
        h = h + g2 * peer_ffn(hn, peer_w_query[l], peer_sub_keys[l], peer_u[l], peer_v[l])
    return rmsnorm(h, final_norm_g)
```

```python
import numpy as np
from contextlib import ExitStack
import concourse.bass as bass
import concourse.mybir as mybir
from concourse.bass_utils import run_bass_kernel_spmd

F32 = mybir.dt.float32
BF16 = mybir.dt.bfloat16
U32 = mybir.dt.uint32
AF = mybir.ActivationFunctionType
ALU = mybir.AluOpType
AX = mybir.AxisListType

class Sched:
    ENG = ("pe", "dve", "act", "pool", "sp")
    EPOCH = 30000
    NDMA = 24

    def __init__(self, nc):
        self.nc = nc
        self.ops = {e: [] for e in self.ENG}
        self.res = {}
        self.dma_rr = {e: 0 for e in self.ENG}
        self.dma_last = {e: [None] * self.NDMA for e in self.ENG}

    def issue(self, eng, fn, reads=(), writes=(), dma=False):
        deps = set()
        for k in reads:
            r = self.res.get(k)
            if r is not None and r["w"] is not None:
                deps.add((r["w"], "raw"))
        for k in writes:
            r = self.res.get(k)
            if r is not None:
                if r["w"] is not None:
                    deps.add((r["w"], "waw"))
                for x in r["r"]:
                    deps.add((x, "war"))
        idx = len(self.ops[eng])
        me = (eng, idx)
        slot = None
        if dma:
            slot = self.dma_rr[eng] % self.NDMA
            self.dma_rr[eng] += 1
            prev = self.dma_last[eng][slot]
            if prev is not None:
                deps.add((prev, "raw"))
            self.dma_last[eng][slot] = me
        keep = set()
        for (p, kind) in deps:
            pe_, pi_ = p
            prod = self.ops[pe_][pi_]
            if (not prod["dma"]) and (not dma) and pe_ == eng:
                if eng == "pe":
                    continue
            keep.add(p)
        self.ops[eng].append(dict(fn=fn, deps=keep, dma=dma, slot=slot, inc=False))
        for k in reads:
            r = self.res.setdefault(k, dict(w=None, r=[]))
            r["r"].append(me)
        for k in writes:
            self.res[k] = dict(w=me, r=[])
        return me

    def barrier(self):
        lasts = set()
        for e in self.ENG:
            for i in range(len(self.ops[e]) - 1, -1, -1):
                op = self.ops[e][i]
                if op["fn"] is not None and not op["dma"]:
                    lasts.add((e, i))
                    break
            for prev in self.dma_last[e]:
                if prev is not None:
                    lasts.add(prev)
        for e in self.ENG:
            self.ops[e].append(dict(fn=None, deps=set(lasts), dma=False, slot=None, inc=False))

    def wait_all(self, eng, keys):
        return self.issue(eng, None, reads=list(keys), writes=())

    def emit(self, stack):
        nc = self.nc
        for e in self.ENG:
            for op in self.ops[e]:
                for (pe_, pi_) in op["deps"]:
                    self.ops[pe_][pi_]["inc"] = True
        nsem_eng = {}
        for e in self.ENG:
            cnt = 0
            dcnt = [0] * self.NDMA
            for op in self.ops[e]:
                if op["dma"]:
                    dcnt[op["slot"]] += 16
                    op["sem"] = ("d", e, op["slot"])
                    op["val"] = dcnt[op["slot"]]
                elif op["inc"]:
                    ep = cnt // self.EPOCH
                    cnt += 1
                    op["sem"] = ("c", e, ep)
                    op["val"] = cnt - ep * self.EPOCH
            nsem_eng[e] = cnt // self.EPOCH + 1
        sems = {}
        for e in self.ENG:
            if any((not op["dma"]) and op["inc"] for op in self.ops[e]):
                for ep in range(nsem_eng[e]):
                    sems[("c", e, ep)] = stack.enter_context(nc.semaphore(f"c_{e}_{ep}"))
            used = set(op["slot"] for op in self.ops[e] if op["dma"])
            for s in used:
                sems[("d", e, s)] = stack.enter_context(nc.semaphore(f"d_{e}_{s}"))
        ops = self.ops

        def run(e, engine):
            seen = {}
            for op in ops[e]:
                need = {}
                for (pe_, pi_) in op["deps"]:
                    p = ops[pe_][pi_]
                    s, v = p["sem"], p["val"]
                    if need.get(s, 0) < v:
                        need[s] = v
                for s, v in need.items():
                    if seen.get(s, 0) < v:
                        engine.wait_ge(sems[s], v)
                        seen[s] = v
                if op["fn"] is None:
                    continue
                inst = op["fn"](engine)
                if op["dma"]:
                    inst.then_inc(sems[op["sem"]], 16)
                elif op["inc"]:
                    inst.then_inc(sems[op["sem"]], 1)

        with nc.Block() as block:
            @block.tensor
            def _(eng):
                run("pe", eng)

            @block.vector
            def _(eng):
                run("dve", eng)

            @block.scalar
            def _(eng):
                run("act", eng)

            @block.gpsimd
            def _(eng):
                run("pool", eng)

            @block.sync
            def _(eng):
                run("sp", eng)


D = 1024
NV = 252
NC = 912
EPS = 1e-6
NEG = -1e30


def build(S_TOK, dbg=False):
    nc = bass.Bass("TRN2", target_bir_lowering=False)
    NB1 = 256
    TB = 256
    nblk1 = S_TOK // NB1
    nblk2 = S_TOK // TB

    def din(name, shape, dt=F32):
        return nc.dram_tensor(name, shape, dt, kind="ExternalInput").ap()

    xT = din("xT", [D, S_TOK])
    cT = din("cT", [128, 8])
    ada_w = din("ada_w", [D, 6144])
    vecs_d = din("vecs", [128, NV])
    hp_d = din("hp", [128, 536])
    w_in = din("w_in", [D, 2568])
    w_out = din("w_out", [D, D])
    w_q = din("w_q", [16, 128, 1024])
    keysT = din("keysT", [128, 2048])
    UT = din("UT", [128, 128, 1024])
    VR = din("VR", [128, 128, 1024])
    cst_d = din("cst", [128, NC])
    UTb = nc.dram_tensor("UTb", [128, 128, 1024], BF16, kind="Internal").ap()
    VRb = nc.dram_tensor("VRb", [128, 128, 1024], BF16, kind="Internal").ap()
    WQb = nc.dram_tensor("WQb", [16, 128, 1024], BF16, kind="Internal").ap()
    if dbg:
        h1s = nc.dram_tensor("h1s", [D, S_TOK], F32, kind="ExternalOutput").ap()
    else:
        h1s = nc.dram_tensor("h1s", [D, S_TOK], F32, kind="Internal").ap()
    outT = nc.dram_tensor("outT", [D, S_TOK], F32, kind="ExternalOutput").ap()

    with ExitStack() as st:
        S = Sched(nc)

        def sb(n, s, d=F32):
            return st.enter_context(nc.sbuf_tensor(n, s, d))

        def I(eng, name, reads, writes, *a, **kw):
            return S.issue(eng, lambda e: getattr(e, name)(*a, **kw), reads, writes)

        def DMA(eng, out, in_, reads, writes):
            return S.issue(eng, lambda e: e.dma_start(out=out, in_=in_), reads, writes, dma=True)

        def MM(out, lhsT, rhs, start, stop, reads, writes):
            return S.issue("pe", lambda e: e.matmul(out, lhsT=lhsT, rhs=rhs, start=start, stop=stop), reads, writes)

        def TR(out, in_, ident, reads, writes):
            return S.issue("pe", lambda e: e.transpose(out=out, in_=in_, identity=ident), reads, writes)

        def ACT(out, in_, func, reads, writes, **kw):
            return S.issue("act", lambda e: e.activation(out=out, in_=in_, func=func, **kw), reads, writes)

        PS = [st.enter_context(nc.psum_tensor(f"ps{i}", [128, 512], F32)) for i in range(8)]
        PK = [f"ps{i}" for i in range(8)]

        NA32 = 51300
        arena = sb("arena", [128, NA32])
        aptr = {1: 0, 2: 0}

        def carve(which, shape, dt=F32):
            n = 1
            for q in shape[1:]:
                n *= q
            esz = 4 if dt in (F32, U32) else 2
            n32 = (n * esz + 3) // 4
            off = aptr[which]
            aptr[which] = off + n32
            assert aptr[which] <= NA32, (which, aptr[which])
            v = arena[:, off:off + n32]
            if dt != F32:
                v = v.bitcast(dt)[:, 0:n]
            if len(shape) == 3:
                v = v.rearrange("p (a b) -> p a b", a=shape[1])
            elif len(shape) == 4:
                v = v.rearrange("p (a b c) -> p a b c", a=shape[1], b=shape[2])
            return v

        pers = sb("pers", [128, 1900])
        pptr = [0]

        def PS_(name, shape, dt=F32):
            n = 1
            for q in shape[1:]:
                n *= q
            esz = 4 if dt in (F32, U32) else 2
            n32 = (n * esz + 3) // 4
            off = pptr[0]
            pptr[0] = off + n32
            assert pptr[0] <= 1900
            v = pers[:, off:off + n32]
            if dt != F32:
                v = v.bitcast(dt)[:, 0:n]
            return T(v, name)

        class T:
            def __init__(self, v, name):
                self.v = v
                self.name = name
            def __getitem__(self, k):
                return self.v[k]

        def A1(name, shape, dt=F32):
            return T(carve(1, shape, dt), name)

        def A2(name, shape, dt=F32):
            return T(carve(2, shape, dt), name)

        cst = PS_("cst_s", [128, 400])
        vecs = PS_("vecs_s", [128, NV])
        DMA("sp", cst[:, 0:256], cst_d[:, 0:256], [], ["cst"])
        DMA("sp", cst[:, 256:400], cst_d[:, 768:912], [], ["cst"])
        DMA("sp", vecs[:], vecs_d, [], ["vecs"])
        ident = cst[:, 0:128]
        iota_f = cst[:, 128:256]
        ones = cst[:, 256:384]
        iota16 = cst[:, 384:400]
        identb = PS_("identb", [128, 128], BF16)
        iotab = PS_("iotab", [128, 128], BF16)
        I("dve", "tensor_copy", ["cst"], ["identb"], out=identb[:], in_=ident)
        I("dve", "tensor_copy", ["cst"], ["iotab"], out=iotab[:], in_=iota_f)
        n1g, n2g, fng = vecs[:, 0:8], vecs[:, 8:16], vecs[:, 16:24]
        adab = vecs[:, 24:72]

        aptr[3] = 23400
        adw = [carve(3, [128, 8, 768]) for i in range(2)]
        ada_v = ada_w.rearrange("(dc p) f -> p dc f", p=128)
        cTs = PS_("cTs", [128, 8])
        cond = PS_("cond", [128, 8])
        DMA("sp", cTs[:], cT, [], ["cTs"])
        ACT(cond[:], cTs[:], AF.Silu, ["cTs"], ["cond"])
        modT = PS_("modT", [128, 48])
        for pc in range(8):
            b = pc % 2
            DMA("sp", adw[b], ada_v[:, :, pc * 768:(pc + 1) * 768], [], [("adw", b)])
            for f6 in range(6):
                fc = pc * 6 + f6
                for dc in range(8):
                    MM(PS[0][:, fc:fc + 1], adw[b][:, dc, f6 * 128:(f6 + 1) * 128], cond[:, dc:dc + 1],
                       dc == 0, dc == 7, [("adw", b), "cond"], [PK[0]])
        I("dve", "tensor_tensor", [PK[0], "vecs"], ["modT"], out=modT[:], in0=PS[0][:, 0:48], in1=adab, op=ALU.add)
        sh1, sc1, g1 = modT[:, 0:8], modT[:, 8:16], modT[:, 16:24]
        sh2, sc2, g2 = modT[:, 24:32], modT[:, 32:40], modT[:, 40:48]
        gm = PS_("gm", [128, 16])
        I("dve", "tensor_scalar", ["modT"], ["gm"], out=gm[:, 0:8], in0=sc1, scalar1=1.0, scalar2=None, op0=ALU.add)
        I("dve", "tensor_scalar", ["modT"], ["gm"], out=gm[:, 8:16], in0=sc2, scalar1=1.0, scalar2=None, op0=ALU.add)
        I("dve", "tensor_tensor", ["gm", "vecs"], ["gm"], out=gm[:, 0:8], in0=gm[:, 0:8], in1=n1g, op=ALU.mult)
        I("dve", "tensor_tensor", ["gm", "vecs"], ["gm"], out=gm[:, 8:16], in0=gm[:, 8:16], in1=n2g, op=ALU.mult)
        gm1, gm2 = gm[:, 0:8], gm[:, 8:16]

        winb = A1("winb", [128, 8, 2568], BF16)
        woutb = A1("woutb", [128, 8, 1024], BF16)
        dg = A1("dg", [128, 124, 128], BF16)
        cst1 = A1("cst1", [128, 512])
        hp = A1("hp", [128, 536])
        assert aptr[1] <= 23400
        DMA("sp", cst1[:], cst_d[:, 256:768], [], ["cst"])
        DMA("sp", hp[:], hp_d, [], ["hp"])
        tri = cst1[:, 0:128]
        blk = cst1[:, 128:256]
        chsel = [cst1[:, 256:384], cst1[:, 384:512]]
        stg_big = [carve(3, [128, 2568]) for i in range(2)]
        w_in_v = w_in.rearrange("(dc p) n -> p dc n", p=128)
        w_out_v = w_out.rearrange("(dc p) n -> p dc n", p=128)
        ci = 0
        for (src_v, dst, n) in ((w_in_v, winb, 2568), (w_out_v, woutb, 1024)):
            for dc in range(8):
                b = ci % 2
                DMA("act", stg_big[b][:, 0:n], src_v[:, dc, :], [], [("stg", b)])
                if ci % 2 == 0:
                    I("dve", "tensor_copy", [("stg", b)], [dst.name], out=dst[:, dc, :], in_=stg_big[b][:, 0:n])
                else:
                    ACT(dst[:, dc, :], stg_big[b][:, 0:n], AF.Copy, [("stg", b)], [dst.name])
                ci += 1
        for j in range(124):
            eng = "dve"
            I(eng, "tensor_scalar", ["cst", "vecs"], ["dg"], out=dg[:, j, :], in0=ident,
              scalar1=vecs[:, 116 + j:117 + j], scalar2=None, op0=ALU.mult)

        UT2 = UT.rearrange("a b c -> (a b) c").rearrange("(r q) c -> r (q c)", q=8)
        UTb2 = UTb.rearrange("a b c -> (a b) c").rearrange("(r q) c -> r (q c)", q=8)
        VR2 = VR.rearrange("a b c -> (a b) c").rearrange("(r q) c -> r (q c)", q=8)
        VRb2 = VRb.rearrange("a b c -> (a b) c").rearrange("(r q) c -> r (q c)", q=8)
        DMA("pool", WQb.rearrange("a b c -> (a b) c"), w_q.rearrange("a b c -> (a b) c"), [], ["WQb"])
        cast_jobs = []
        for i in range(16):
            cast_jobs.append((UTb2[i * 128:(i + 1) * 128, :], UT2[i * 128:(i + 1) * 128, :], ("UTb", i)))
            cast_jobs.append((VRb2[i * 128:(i + 1) * 128, :], VR2[i * 128:(i + 1) * 128, :], ("VRb", i)))

        aneg = PS_("aneg", [128, 8])
        ACT(aneg[:], hp[:, 8:16], AF.Exp, ["hp"], ["aneg"])
        I("dve", "tensor_scalar", ["aneg"], ["aneg"], out=aneg[:], in0=aneg[:], scalar1=-1.0, scalar2=None, op0=ALU.mult)
        dskip = hp[:, 24:536]

        S.barrier()
        tmpf_t = PS_("tmpf_t", [128, 512])
        tmpf = None
        rstd = PS_("rstd", [128, 512])
        rstd_full = rstd
        tmpf = [T(tmpf_t[:, 0:NB1], "tmpf"), T(tmpf_t[:, NB1:2 * NB1], "tmpf")]
        rstd = T(rstd_full[:, 0:NB1], "rstd")
        acc = tmpf
        xt = [A1(f"xt{i}", [128, 8, NB1]) for i in range(3)]
        hns = [A1(f"hn{i}", [128, 8, NB1], BF16) for i in range(2)]
        zss = [A1(f"zs{i}", [128, 4, NB1]) for i in range(2)]
        xr = A1("xr", [128, 8, 3 + NB1])
        xbcss = [A1(f"xbcs{i}", [128, 8, NB1], BF16) for i in range(2)]
        uin = A1("uin", [128, 4, 30 + NB1], BF16)
        ucv = A1("ucv", [128, 4, NB1])
        ycats = [A1(f"ycat{i}", [128, 8, NB1], BF16) for i in range(2)]
        yg = A1("yg", [128, 4, NB1])
        tmpfB = [A1(f"tmpfB{i}", [128, NB1]) for i in range(2)]
        rstdB = rstd_full[:, NB1:2 * NB1]
        mean = A1("mean", [128, NB1])
        Hf = A1("Hf", [128, 512])
        Hb = [A1(f"Hb{i}", [128, 512], BF16) for i in range(2)]
        sm = A1("sm", [128, 96])
        acs_sb = A1("acs_sb", [128, 8])
        decb = A1("decb", [128, 16])
        arg = A1("arg", [128, 8, 128])
        ex = A1("ex", [128, 8, 128])
        gmk = A1("gmk", [128, 2, 128])
        scb = A1("scb", [128, 8, 128], BF16)
        tok = A1("tok", [128, 768], BF16)
        xw = A1("xw", [128, 512], BF16)
        xd = A1("xd", [128, 512], BF16)
        ytmp = A1("ytmp", [128, 512])

        I("dve", "memset", [], ["xr"], xr[:], 0.0)
        I("dve", "memset", [], ["uin"], uin[:], 0.0)
        I("dve", "memset", [], ["Hf"], Hf[:], 0.0)
        I("dve", "memset", [], ["Hb0"], Hb[0][:], 0.0)

        xT_v = xT.rearrange("(dc p) t -> p dc t", p=128)
        h1_v = h1s.rearrange("(dc p) t -> p dc t", p=128)
        out_v = outT.rearrange("(dc p) t -> p dc t", p=128)

        def rstd_from(ps_ap, n, scale, reads_extra=(), dst=None, key="rstd"):
            d_ = rstd[:, 0:n] if dst is None else dst
            I("dve", "tensor_scalar", list(reads_extra), [key], out=d_, in0=ps_ap, scalar1=scale, scalar2=EPS,
              op0=ALU.mult, op1=ALU.add)
            ACT(d_, d_, AF.Ln, [key], [key])
            ACT(d_, d_, AF.Exp, [key], [key], scale=-0.5)

        DMA("sp", xt[0][:], xT_v[:, :, 0:NB1], [], ["xt0"])
        ncj = (32 + nblk1 - 1) // nblk1

        def stageA(blkI):
            pr = blkI % 2
            X = xt[blkI % 3]
            XK = f"xt{blkI % 3}"
            t0 = blkI * NB1
            hn = hns[pr]; HNK = f"hn{pr}"
            zs = zss[pr]; ZK = f"zs{pr}"
            xbcs = xbcss[pr]
            ycat = ycats[pr]
            if blkI + 1 < nblk1:
                DMA("sp", xt[(blkI + 1) % 3][:], xT_v[:, :, t0 + NB1:t0 + 2 * NB1], [], [f"xt{(blkI + 1) % 3}"])
            for cj in cast_jobs[blkI * ncj:(blkI + 1) * ncj]:
                DMA("pool", cj[0], cj[1], [], [cj[2]])
            yield
            for dc in range(8):
                tb = dc % 2
                ACT(tmpf[tb][:], X[:, dc, :], AF.Square, [XK], [("tmpf", tb)])
                MM(PS[0][:, 0:NB1], ones, tmpf[tb][:], dc == 0, dc == 7, [("tmpf", tb), "cst"], [PK[0]])
            rstd_from(PS[0][:, 0:NB1], NB1, 1.0 / D, [PK[0]])
            yield
            for dc in range(8):
                tb = dc % 2
                I("dve", "scalar_tensor_tensor", [XK, "gm", "rstd"], [("tmpf", tb)], out=tmpf[tb][:], in0=X[:, dc, :],
                  scalar=gm1[:, dc:dc + 1], in1=rstd[:, :], op0=ALU.mult, op1=ALU.mult)
                ACT(hn[:, dc, :], tmpf[tb][:], AF.Identity, [("tmpf", tb), "modT"], [HNK], bias=sh1[:, dc:dc + 1])
                if dc == 3:
                    yield
            yield

            def proj(col0, ncols, bank):
                for dc in range(8):
                    MM(PS[bank][0:ncols, 0:NB1], winb[:, dc, col0:col0 + ncols], hn[:, dc, :], dc == 0, dc == 7,
                       ["winb", HNK], [PK[bank]])

            for j in range(4):
                bank = j % 2
                proj(j * 128, 128, bank)
                ACT(zs[:, j, :], PS[bank][:, 0:NB1], AF.Silu, [PK[bank]], [ZK])
                if j % 2 == 1:
                    yield
            for j in range(8):
                bank = j % 2
                if blkI > 0:
                    I("dve", "tensor_copy", [("xr", j)], [("xr", j)], out=xr[:, j, 0:3], in_=xr[:, j, NB1:NB1 + 3])
                proj(512 + j * 128, 128, bank)
                ACT(xr[:, j, 3:3 + NB1], PS[bank][:, 0:NB1], AF.Copy, [PK[bank]], [("xr", j)])
                a = acc[j % 2]
                ak = ("tmpf", j % 2)
                I("dve", "tensor_scalar", [("xr", j), "vecs"], [ak], out=a[:], in0=xr[:, j, 3:3 + NB1],
                  scalar1=vecs[:, 72 + 3 * 8 + j:72 + 3 * 8 + j + 1], scalar2=None, op0=ALU.mult)
                for k in range(3):
                    I("dve", "scalar_tensor_tensor", [("xr", j), "vecs", ak], [ak], out=a[:], in0=xr[:, j, k:k + NB1],
                      scalar=vecs[:, 72 + k * 8 + j:72 + k * 8 + j + 1], in1=a[:], op0=ALU.mult, op1=ALU.add)
                ACT(xbcs[:, j, :], a[:], AF.Silu, [ak, "vecs"], [("xbcs", pr, j)], bias=vecs[:, 104 + j:105 + j])
                yield
            for j in range(4):
                if blkI > 0:
                    I("dve", "tensor_copy", [("uin", j)], [("uin", j)], out=uin[:, j, 0:30], in_=uin[:, j, NB1:NB1 + 30])
                proj(2056 + j * 128, 128, 0)
                proj(1544 + j * 128, 128, 1)
                tb = j % 2
                ACT(tmpf[tb][:], PS[0][:, 0:NB1], AF.Sigmoid, [PK[0]], [("tmpf", tb)])
                I("dve", "tensor_tensor", [PK[1], ("tmpf", tb)], [("uin", j)], out=uin[:, j, 30:30 + NB1], in0=PS[1][:, 0:NB1],
                  in1=tmpf[tb][:], op=ALU.mult)
                yield
                for k in range(31):
                    MM(PS[0][:, 0:NB1], dg[:, j * 31 + k, :], uin[:, j, k:k + NB1], k == 0, k == 30, ["dg", ("uin", j)], [PK[0]])
                    if k == 15:
                        pass
                ACT(ucv[:, j, :], PS[0][:, 0:NB1], AF.Identity, [PK[0], "vecs"], [("ucv", j)], bias=vecs[:, 240 + j:241 + j])
                yield
            for j in range(4):
                MM(PS[0][:, 0:NB1], ones, ucv[:, j, :], j == 0, j == 3, [("ucv", j), "cst"], [PK[0]])
            for j in range(4):
                tb = j % 2
                ACT(tmpf[tb][:], ucv[:, j, :], AF.Square, [("ucv", j)], [("tmpf", tb)])
                MM(PS[1][:, 0:NB1], ones, tmpf[tb][:], j == 0, j == 3, [("tmpf", tb), "cst"], [PK[1]])
            I("dve", "tensor_scalar", [PK[0]], ["mean"], out=mean[:], in0=PS[0][:, 0:NB1], scalar1=1.0 / 512, scalar2=None, op0=ALU.mult)
            I("dve", "tensor_tensor", ["mean"], [("tmpf", 0)], out=tmpf[0][:], in0=mean[:], in1=mean[:], op=ALU.mult)
            I("dve", "scalar_tensor_tensor", [PK[1], ("tmpf", 0)], [("tmpf", 1)], out=tmpf[1][:], in0=PS[1][:, 0:NB1], scalar=1.0 / 512,
              in1=tmpf[0][:], op0=ALU.mult, op1=ALU.subtract)
            rstd_from(tmpf[1][:], NB1, 1.0, [("tmpf", 1)])
            yield
            for j in range(4):
                tb = j % 2
                I("dve", "tensor_tensor", [("ucv", j), "mean"], [("tmpf", tb)], out=tmpf[tb][:], in0=ucv[:, j, :], in1=mean[:], op=ALU.subtract)
                I("dve", "tensor_tensor", [("tmpf", tb), "rstd"], [("tmpf", tb)], out=tmpf[tb][:], in0=tmpf[tb][:], in1=rstd[:, :], op=ALU.mult)
                ACT(ycat[:, 4 + j, :], tmpf[tb][:], AF.Silu, [("tmpf", tb), "vecs"], [("ycat", pr, 4 + j)],
                    scale=vecs[:, 244 + j:245 + j], bias=vecs[:, 248 + j:249 + j])
                if j == 1:
                    yield
            yield

        def stageB(blkI):
            pr = blkI % 2
            X = xt[blkI % 3]
            XK = f"xt{blkI % 3}"
            t0 = blkI * NB1
            hn = hns[pr]; HNK = f"hn{pr}"
            zs = zss[pr]; ZK = f"zs{pr}"
            xbcs = xbcss[pr]
            ycat = ycats[pr]
            XB = lambda j: ("xbcs", pr, j)
            YGK = ["yg"]
            for sbI in range(NB1 // 128):
                T0 = sbI * 128
                TS = slice(T0, T0 + 128)
                for dc in range(8):
                    MM(PS[6][:, 0:8], hn[:, dc, TS], winb[:, dc, 1536:1544], dc == 0, dc == 7, [HNK, "winb"], [PK[6]])
                xb_ = sm[:, 0:8]; ax = sm[:, 8:16]; ee = sm[:, 16:24]; dtv = sm[:, 24:32]; dA = sm[:, 32:40]
                eacs = sm[:, 40:48]; toend = sm[:, 48:56]; tt_ = sm[:, 56:64]
                I("dve", "tensor_tensor", [PK[6], "hp"], ["sm"], out=xb_, in0=PS[6][:, 0:8], in1=hp[:, 0:8], op=ALU.add)
                ACT(ee, xb_, AF.Exp, ["sm"], ["sm"])
                ACT(dtv, ee, AF.Ln, ["sm", "cst"], ["sm"], bias=ones[:, 0:1])
                yield
                I("dve", "tensor_tensor", ["sm", "aneg"], ["sm"], out=dA, in0=dtv, in1=aneg[:], op=ALU.mult)
                MM(PS[6][:, 8:16], tri, dA, True, True, ["sm", "cst"], [PK[6]])
                MM(PS[6][:, 16:24], blk, dA, True, True, ["sm", "cst"], [PK[6]])
                MM(PS[6][:, 24:32], chsel[0], dA, True, True, ["sm", "cst"], [PK[6]])
                MM(PS[6][:, 32:40], chsel[1], dA, True, True, ["sm", "cst"], [PK[6]])
                for h in range(8):
                    bank = 4 + h // 4
                    MM(PS[bank][:, (h % 4) * 128:(h % 4 + 1) * 128], dA[:, h:h + 1].to_broadcast([128, 128]), tri, True, True,
                       ["sm", "cst"], [PK[bank]])
                yield
                I("dve", "tensor_copy", [PK[6]], ["acs_sb"], out=acs_sb[:], in_=PS[6][:, 8:16])
                ACT(eacs, PS[6][:, 8:16], AF.Exp, [PK[6]], ["sm2"])
                I("dve", "tensor_tensor", [PK[6], "acs_sb"], ["sm2"], out=tt_, in0=PS[6][:, 16:24], in1=acs_sb[:], op=ALU.subtract)
                ACT(tt_, tt_, AF.Exp, ["sm2"], ["sm2"])
                I("dve", "tensor_tensor", ["sm2", "sm"], ["sm2"], out=toend, in0=tt_, in1=dtv, op=ALU.mult)
                ACT(decb[:], PS[6][:, 24:40], AF.Exp, [PK[6]], ["decb"])
                yield
                for hh in range(2):
                    bank = 4 + hh
                    I("dve", "tensor_tensor", [PK[bank], "acs_sb"], [("arg", hh)], out=arg[:, hh * 4:(hh + 1) * 4, :],
                      in0=PS[bank][:, :].rearrange("p (h t) -> p h t", h=4),
                      in1=acs_sb[:, hh * 4:(hh + 1) * 4].unsqueeze(2).to_broadcast([128, 4, 128]), op=ALU.subtract)
                    I("dve", "tensor_scalar", [("arg", hh)], [("arg", hh)], out=arg[:, hh * 4:(hh + 1) * 4, :],
                      in0=arg[:, hh * 4:(hh + 1) * 4, :], scalar1=0.0, scalar2=None, op0=ALU.min)
                    ACT(ex[:, hh * 4:(hh + 1) * 4, :], arg[:, hh * 4:(hh + 1) * 4, :], AF.Exp, [("arg", hh)], [("ex", hh)])
                for g in range(2):
                    MM(PS[6][:, 128 + g * 128:256 + g * 128], xbcs[:, 4 + g, TS], xbcs[:, 6 + g, TS], True, True,
                       [XB(4 + g), XB(6 + g)], [PK[6]])
                yield
                I("dve", "tensor_tensor", [PK[6], "cst"], ["gmk"], out=gmk[:], in0=PS[6][:, 128:384].rearrange("p (g t) -> p g t", g=2),
                  in1=tri.unsqueeze(1).to_broadcast([128, 2, 128]), op=ALU.mult)
                for g in range(2):
                    I("dve", "tensor_tensor", [("ex", g), "gmk"], [("ex", g)], out=ex[:, g * 4:(g + 1) * 4, :], in0=ex[:, g * 4:(g + 1) * 4, :],
                      in1=gmk[:, g:g + 1, :].to_broadcast([128, 4, 128]), op=ALU.mult)
                I("dve", "tensor_tensor", [("ex", 0), ("ex", 1), "sm"], ["scb"], out=scb[:], in0=ex[:],
                  in1=dtv.unsqueeze(2).to_broadcast([128, 8, 128]), op=ALU.mult)
                trp = PS[3][:, :].bitcast(BF16)
                for j in range(6):
                    TR(trp[:, j * 128:(j + 1) * 128], xbcs[:, j, TS], identb[:], [XB(j), "identb"], [PK[3]])
                ACT(tok[:], trp[:, 0:768], AF.Copy, [PK[3]], ["tok"])
                yield
                I("dve", "tensor_tensor", ["tok", "sm2"], ["xw"], out=xw[:].rearrange("p (h q) -> p h q", h=8),
                  in0=tok[:, 0:512].rearrange("p (h q) -> p h q", h=8), in1=toend.unsqueeze(2).to_broadcast([128, 8, 64]), op=ALU.mult)
                I("pool", "tensor_tensor", ["tok", "hp"], ["xd"], out=xd[:], in0=tok[:, 0:512], in1=dskip, op=ALU.mult)
                for h in range(8):
                    hs = slice(h * 64, (h + 1) * 64)
                    MM(PS[2][:, hs], scb[:, h, :], tok[:, hs], True, False, ["scb", "tok"], [PK[2]])
                    MM(PS[2][:, hs], identb[:], xd[:, hs], False, True, ["identb", "xd"], [PK[2]])
                yield
                for c in range(2):
                    R = slice(c * 64, (c + 1) * 64)
                    hb_in = Hb[c]
                    hb_out = Hb[(c + 1) % 2]
                    for g in range(2):
                        gs = slice(g * 256, (g + 1) * 256)
                        MM(PS[7][R, gs], xbcs[:, 6 + g, T0 + c * 64:T0 + (c + 1) * 64], hb_in[:, gs], True, True,
                           [XB(6 + g), f"Hb{c}"], [PK[7]])
                        MM(PS[3][:, gs], tok[R, 512 + g * 128:512 + (g + 1) * 128], xw[R, gs], True, True, ["tok", "xw"], [PK[3]])
                    I("dve", "tensor_tensor", ["Hf", "decb"], ["Hf"], out=Hf[:].rearrange("p (h q) -> p h q", h=8),
                      in0=Hf[:].rearrange("p (h q) -> p h q", h=8),
                      in1=decb[:, c * 8:(c + 1) * 8].unsqueeze(2).to_broadcast([128, 8, 64]), op=ALU.mult)
                    I("dve", "tensor_tensor", ["Hf", PK[3]], ["Hf"], out=Hf[:], in0=Hf[:], in1=PS[3][:, :], op=ALU.add)
                    ACT(hb_out[:], Hf[:], AF.Copy, ["Hf"], [f"Hb{(c + 1) % 2}"])
                    I("dve", "tensor_tensor", [PK[7], "sm2"], ["ytmp"], out=ytmp[R, :].rearrange("p (h q) -> p h q", h=8),
                      in0=PS[7][R, :].rearrange("p (h q) -> p h q", h=8), in1=eacs[R, :].unsqueeze(2).to_broadcast([64, 8, 64]), op=ALU.mult)
                    yield
                I("dve", "tensor_tensor", ["ytmp", PK[2]], ["ytmp"], out=ytmp[:], in0=ytmp[:], in1=PS[2][:, :], op=ALU.add)
                for j in range(4):
                    TR(PS[2][:, j * 128:(j + 1) * 128], ytmp[:, j * 128:(j + 1) * 128], ident, ["ytmp", "cst"], [PK[2]])
                I("dve", "tensor_tensor", [PK[2], ZK], YGK, out=yg[:, :, TS], in0=PS[2][:, :].rearrange("p (j t) -> p j t", j=4),
                  in1=zs[:, :, TS], op=ALU.mult)
                yield
            for j in range(4):
                tb = j % 2
                ACT(tmpfB[tb][:], yg[:, j, :], AF.Square, YGK, [("tmpfB", tb)])
                MM(PS[7][:, 0:NB1], ones, tmpfB[tb][:], j == 0, j == 3, [("tmpfB", tb), "cst"], [PK[7]])
            rstd_from(PS[7][:, 0:NB1], NB1, 1.0 / 512, [PK[7]], dst=rstdB, key="rstdB")
            yield
            for j in range(4):
                I("dve", "scalar_tensor_tensor", YGK + ["vecs", "rstdB"], [("ycat", pr, j)], out=ycat[:, j, :], in0=yg[:, j, :],
                  scalar=vecs[:, 112 + j:113 + j], in1=rstdB, op0=ALU.mult, op1=ALU.mult)
            yield
            ROTB = (2, 3, 4, 5)
            for dcn in range(8):
                bank = ROTB[dcn % 4]
                for kc in range(8):
                    MM(PS[bank][:, 0:NB1], woutb[:, kc, dcn * 128:(dcn + 1) * 128], ycat[:, kc, :], kc == 0, kc == 7,
                       ["woutb"] + [("ycat", pr, q) for q in range(8)], [PK[bank]])
                I("dve", "scalar_tensor_tensor", [PK[bank], "modT", XK], [XK], out=X[:, dcn, :], in0=PS[bank][:, 0:NB1],
                  scalar=g1[:, dcn:dcn + 1], in1=X[:, dcn, :], op0=ALU.mult, op1=ALU.add)
                if dcn % 2 == 1:
                    yield
            DMA("sp", h1_v[:, :, t0:t0 + NB1], X[:], [XK], ["h1s"])
            yield

        for _ in stageA(0):
            pass
        for blkI in range(nblk1):
            gB = stageB(blkI)
            gA = stageA(blkI + 1) if blkI + 1 < nblk1 else None
            doneA = gA is None
            doneB = False
            while not (doneA and doneB):
                if not doneB:
                    try:
                        next(gB)
                    except StopIteration:
                        doneB = True
                if not doneA:
                    try:
                        next(gA)
                    except StopIteration:
                        doneA = True

        S.barrier()
        rstd = rstd_full
        TB = 256
        nblk2 = S_TOK // TB
        NT = TB // 128
        NG = TB // 4
        ABufs = [carve(2, [128, 128, TB], BF16) for _ in range(2)]
        hn2s = [A2(f"hn2{i}", [128, 8, TB], BF16) for i in range(2)]
        keysb = A2("keysb", [128, 2048], BF16)
        DMA("pool", keysb[:], keysT, [], ["keysb"])
        kgTs = [A2(f"kgT{i}", [128, 3, TB]) for i in range(2)]
        NUB, NVB = 4, 6
        ub = [A2(f"ub{i}", [128, 8, 128], BF16) for i in range(NUB)]
        vb = [A2(f"vb{i}", [128, 1, 1024], BF16) for i in range(NVB)]
        P0 = [T(vb[i][:, 0, 0:512].rearrange("p (a b) -> p a b", a=4), "P0") for i in range(2)]
        P1 = [T(vb[i][:, 0, 512:1024].rearrange("p (a b) -> p a b", a=4), "P1") for i in range(2)]
        aab = [A2(f"aab{i}", [128, 128], BF16) for i in range(2)]
        hb2 = carve(2, [128, 8, TB])
        rs0 = aptr[2]
        sc = A2("sc", [128, 16, 128])
        hbv = arena[:, rs0:rs0 + 2048].rearrange("p (a b) -> p a b", a=8)
        cand = A2("cand", [128, 4, 256])
        qT = A2("qT", [128, 16, 128], BF16)
        m16 = A2("m16", [128, 16, 16])
        i16 = A2("i16", [128, 16, 16], U32)
        i16f = A2("i16f", [128, 16, 16])
        b16 = A2("b16", [128, 8, 16])
        p16 = A2("p16", [128, 8, 16], U32)
        pab = A2("pab", [128, 2, 128], U32)
        pabf = A2("pabf", [128, 2, 128])
        kgf2 = [A2(f"kgf{i}", [128, 3, 128]) for i in range(2)]
        zz = A2("zz", [128, 16])
        eq = T(sc[:].rearrange("p a k -> p (a k)").rearrange("p (x y) -> p x y", y=16), "eq")
        tmpA = tmpf_t[:, 0:256]
        tmpB = tmpf_t[:, 256:512]
        SCK = [("sc", q) for q in range(16)]
        CANDK = [("cand", h) for h in range(4)]
        M16K = [("m16", q, v) for q in range(16) for v in range(2)]
        I16K = [("i16", q, v) for q in range(16) for v in range(2)]
        B16K = [("b16", h, v) for h in range(8) for v in range(2)]
        P16K = [("p16", h, v) for h in range(8) for v in range(2)]
        HBK = SCK
        ABKs = [[("AB", i, q) for q in range(8)] for i in range(2)]
        UTb_v = UTb.rearrange("k p (dc q) -> p k dc q", dc=8)
        VRb_v = VRb.rearrange("k q d -> q k d")
        CERF = 1.0

        def stream(n, nbuf, load_fn, use_fn, after=None):
            for i in range(min(nbuf - 1, n)):
                load_fn(i, i % nbuf)
            for i in range(n):
                if i + nbuf - 1 < n:
                    load_fn(i + nbuf - 1, (i + nbuf - 1) % nbuf)
                use_fn(i, i % nbuf)
                if after is not None:
                    after(i)

        def routing_steps(bb):
            t0 = bb * TB
            hn2 = hn2s[bb % 2]
            hk = f"hn2{bb % 2}"
            kgT = kgTs[bb % 2]
            kk_ = f"kgT{bb % 2}"
            steps = []
            steps.append(lambda: DMA("sp", hbv, h1_v[:, :, t0:t0 + TB], ["h1s"], HBK))

            def st_stats(lo, hi):
                def f():
                    for dc in range(lo, hi):
                        ACT(tmpA, hbv[:, dc, :], AF.Square, HBK, [("tmpf", 0)])
                        MM(PS[4][:, 0:TB], ones, tmpA, dc == 0, dc == 7, [("tmpf", 0), "cst"], [PK[4]])
                return f
            steps.append(st_stats(0, 4))
            steps.append(st_stats(4, 8))
            steps.append(lambda: rstd_from(PS[4][:, 0:TB], TB, 1.0 / D, [PK[4]]))

            def st_hn(lo, hi):
                def f():
                    for dc in range(lo, hi):
                        I("dve", "scalar_tensor_tensor", HBK + ["gm", "rstd"], [("tmpf", 0)], out=tmpA, in0=hbv[:, dc, :],
                          scalar=gm2[:, dc:dc + 1], in1=rstd[:, 0:TB], op0=ALU.mult, op1=ALU.mult)
                        ACT(hn2[:, dc, :], tmpA, AF.Identity, [("tmpf", 0), "modT"], [hk], bias=sh2[:, dc:dc + 1])
                return f
            steps.append(st_hn(0, 4))
            steps.append(st_hn(4, 8))

            def q_steps(tt):
                TS = slice(tt * 128, (tt + 1) * 128)
                out = []

                NQ = NUB - 1

                def ldq(i):
                    bf = i % NQ
                    DMA("sp", ub[bf][:], WQb[i].rearrange("p (dc c) -> p dc c", dc=8), ["WQb"], [f"ub{bf}"])

                def mk(i):
                    def f():
                        if i == 0:
                            for j in range(NQ - 1):
                                ldq(j)
                        if i + NQ - 1 < 16:
                            ldq(i + NQ - 1)
                        bf = i % NQ
                        bank = 4 + (i // 4) % 2
                        for dc in range(8):
                            MM(PS[bank][:, (i % 4) * 128:(i % 4 + 1) * 128], ub[bf][:, dc, :], hn2[:, dc, TS], dc == 0, dc == 7,
                               [f"ub{bf}", hk], [PK[bank]])
                        if i % 4 == 3:
                            g4 = i // 4
                            ACT(qT[:, g4 * 4:(g4 + 1) * 4, :], PS[bank][:, :].rearrange("p (a t) -> p a t", a=4), AF.Copy, [PK[bank]], ["qT"])
                    return f
                for i in range(16):
                    out.append(mk(i))
                return out

            def scores_step(tt):
                def f():
                    for half in range(2):
                        for q8 in range(8):
                            qc = half * 8 + q8
                            bank = 6 + q8 // 4
                            MM(PS[bank][:, (q8 % 4) * 128:(q8 % 4 + 1) * 128], qT[:, qc, :], keysb[:, qc * 128:(qc + 1) * 128], True, True,
                               ["qT", "keysb"], [PK[bank]])
                        for bb_ in range(2):
                            q0 = half * 8 + bb_ * 4
                            ACT(sc[:, q0:q0 + 4, :], PS[6 + bb_][:, :].rearrange("p (a k) -> p a k", a=4), AF.Copy, [PK[6 + bb_]],
                                [("sc", q0 + q) for q in range(4)])
                return f

            def chain_step(tt):
                kgf = kgf2[tt % 2]
                kp = tt % 2

                def f():
                    for qc in range(16):
                        I("dve", "max", [("sc", qc)], [("m16", qc, 0)], out=m16[:, qc, 0:8], in_=sc[:, qc, :])
                    for qc in range(16):
                        I("dve", "max_index", [("sc", qc), ("m16", qc, 0)], [("i16", qc, 0)], out=i16[:, qc, 0:8], in_max=m16[:, qc, 0:8], in_values=sc[:, qc, :])
                    for qc in range(16):
                        I("dve", "match_replace", [("sc", qc), ("m16", qc, 0)], [("sc", qc)], out=sc[:, qc, :], in_to_replace=m16[:, qc, 0:8], in_values=sc[:, qc, :], imm_value=NEG)
                    for qc in range(16):
                        I("dve", "max", [("sc", qc)], [("m16", qc, 1)], out=m16[:, qc, 8:16], in_=sc[:, qc, :])
                    for qc in range(16):
                        I("dve", "max_index", [("sc", qc), ("m16", qc, 1)], [("i16", qc, 1)], out=i16[:, qc, 8:16], in_max=m16[:, qc, 8:16], in_values=sc[:, qc, :])
                    I("dve", "tensor_copy", I16K, ["i16f"], out=i16f[:], in_=i16[:])
                    m16v = m16[:].rearrange("p (h i) a -> p h i a", i=2)
                    i16v = i16f[:].rearrange("p (h i) a -> p h i a", i=2)
                    for hh in range(2):
                        H4 = slice(hh * 4, (hh + 1) * 4)
                        I("dve", "tensor_tensor", M16K, CANDK, out=cand[:].rearrange("p h (a b) -> p h a b", a=16),
                          in0=m16v[:, H4, 0, :].unsqueeze(3).to_broadcast([128, 4, 16, 16]),
                          in1=m16v[:, H4, 1, :].unsqueeze(2).to_broadcast([128, 4, 16, 16]), op=ALU.add)
                        for h4 in range(4):
                            h = hh * 4 + h4
                            I("dve", "max", [("cand", h4)], [("b16", h, 0)], out=b16[:, h, 0:8], in_=cand[:, h4, :])
                        for h4 in range(4):
                            h = hh * 4 + h4
                            I("dve", "max_index", [("cand", h4), ("b16", h, 0)], [("p16", h, 0)], out=p16[:, h, 0:8], in_max=b16[:, h, 0:8], in_values=cand[:, h4, :])
                        for h4 in range(4):
                            h = hh * 4 + h4
                            I("dve", "match_replace", [("cand", h4), ("b16", h, 0)], [("cand", h4)], out=cand[:, h4, :], in_to_replace=b16[:, h, 0:8], in_values=cand[:, h4, :], imm_value=NEG)
                        for h4 in range(4):
                            h = hh * 4 + h4
                            I("dve", "max", [("cand", h4)], [("b16", h, 1)], out=b16[:, h, 8:16], in_=cand[:, h4, :])
                        for h4 in range(4):
                            h = hh * 4 + h4
                            I("dve", "max_index", [("cand", h4), ("b16", h, 1)], [("p16", h, 1)], out=p16[:, h, 8:16], in_max=b16[:, h, 8:16], in_values=cand[:, h4, :])
                    p16f = p16[:].rearrange("p h j -> p (h j)")
                    I("dve", "tensor_single_scalar", P16K, ["pab"], out=pab[:, 0, :], in_=p16f, scalar=4, op=ALU.logical_shift_right)
                    I("dve", "tensor_single_scalar", P16K, ["pab"], out=pab[:, 1, :], in_=p16f, scalar=15, op=ALU.bitwise_and)
                    I("dve", "tensor_copy", ["pab"], ["pabf"], out=pabf[:], in_=pab[:])
                    for i2 in range(2):
                        I("dve", "tensor_tensor", ["pabf", "cst"], SCK, out=eq[:].rearrange("p (h j) a -> p h j a", h=8),
                          in0=iota16.unsqueeze(1).unsqueeze(1).to_broadcast([128, 8, 16, 16]),
                          in1=pabf[:, i2, :].rearrange("p (h j) -> p h j", h=8).unsqueeze(3).to_broadcast([128, 8, 16, 16]), op=ALU.is_equal)
                        I("dve", "tensor_tensor", SCK + ["i16f"], SCK, out=eq[:].rearrange("p (h j) a -> p h j a", h=8),
                          in0=eq[:].rearrange("p (h j) a -> p h j a", h=8),
                          in1=i16v[:, :, i2, :].unsqueeze(2).to_broadcast([128, 8, 16, 16]), op=ALU.mult)
                        I("dve", "tensor_reduce", SCK, [("kgf", kp, i2)], out=kgf[:, i2, :], in_=eq[:], axis=AX.X, op=ALU.add)
                    gv = kgf[:, 2, :].rearrange("p (h j) -> p h j", h=8)
                    I("dve", "tensor_tensor", B16K, [("kgf", kp, 2)], out=gv, in0=b16[:], in1=b16[:, :, 0:1].to_broadcast([128, 8, 16]), op=ALU.subtract)
                    ACT(gv, gv, AF.Exp, [("kgf", kp, 2)], [("kgf", kp, 2)])
                    I("dve", "tensor_reduce", [("kgf", kp, 2)], ["zz"], out=zz[:, 0:8], in_=gv, axis=AX.X, op=ALU.add)
                    I("dve", "tensor_scalar", ["zz"], ["zz"], out=zz[:, 0:8], in0=zz[:, 0:8], scalar1=CERF, scalar2=None, op0=ALU.mult)
                    I("dve", "reciprocal", ["zz"], ["zz"], out=zz[:, 8:16], in_=zz[:, 0:8])
                    I("dve", "tensor_tensor", [("kgf", kp, 2), "zz"], [("kgf", kp, 2)], out=gv, in0=gv, in1=zz[:, 8:16].unsqueeze(2).to_broadcast([128, 8, 16]), op=ALU.mult)
                return f

            def tr_step(tt, bank):
                TS = slice(tt * 128, (tt + 1) * 128)
                kgf = kgf2[tt % 2]
                kp = tt % 2

                def f():
                    KG = [("kgf", kp, q) for q in range(3)]
                    for q in range(3):
                        TR(PS[bank][:, q * 128:(q + 1) * 128], kgf[:, q, :], ident, KG + ["cst"], [PK[bank]])
                    ACT(kgT[:, :, TS], PS[bank][:, 0:384].rearrange("p (q t) -> p q t", q=3), AF.Copy, [PK[bank]], [kk_])
                return f

            return dict(pre=steps, q=[q_steps(tt) for tt in range(NT)], sc=[scores_step(tt) for tt in range(NT)],
                        ch=[chain_step(tt) for tt in range(NT)], tr=[tr_step(tt, 6) for tt in range(NT)], tr_spill=tr_step(NT - 1, 2))

        def run_routing_now(R):
            for f in R["pre"]:
                f()
            for tt in range(NT):
                for f in R["q"][tt]:
                    f()
                R["sc"][tt]()
                R["ch"][tt]()
                R["tr"][tt]()

        def A_phase(bb, after=None):
            AB = ABufs[bb % 2]
            hn2 = hn2s[bb % 2]
            hk = f"hn2{bb % 2}"

            ring = [(ub[i][:], f"ub{i}") for i in range(NUB)] + \
                   [(vb[i][:].rearrange("p a (dc c) -> p (a dc) c", c=128), f"vb{i}") for i in range(2, NVB)]

            def ldu(i, bf):
                DMA("sp", ring[bf][0], UTb_v[:, i, :, :], [("UTb", i // 8)], [ring[bf][1]])

            def useu(k1, bf):
                bank = 4 + k1 % 2
                for dc in range(8):
                    MM(PS[bank][:, 0:TB], ring[bf][0][:, dc, :], hn2[:, dc, :], dc == 0, dc == 7, [ring[bf][1], hk], [PK[bank]])
                ACT(AB[:, k1, :], PS[bank][:, 0:TB], AF.Gelu, [PK[bank]], [("AB", bb % 2, k1 // 16)])
            stream(128, len(ring), ldu, useu, after=after)

        def W_fns(bb):
            AB = ABufs[bb % 2]
            ABK = ABKs[bb % 2]
            kgT = kgTs[bb % 2]
            kk_ = f"kgT{bb % 2}"

            def w_build(tg):
                pb = tg % 2
                xk = [f"vb{pb}"] if tg < 2 else []
                for q in range(4):
                    t = tg * 4 + q
                    if q % 2 == 0:
                        ACT(aab[pb][:], iotab[:], AF.Abs, ["iotab", kk_], [("aab", pb)], scale=-1.0, bias=kgT[:, 0, t:t + 1])
                        ACT(P0[pb][:, q, :], aab[pb][:], AF.Relu, [("aab", pb), "cst"], [("P0", pb, q)] + xk, scale=-1.0, bias=ones[:, 0:1])
                    else:
                        I("dve", "tensor_scalar", ["iotab", kk_], [("P0", pb, q)] + xk, out=P0[pb][:, q, :], in0=iotab[:], scalar1=kgT[:, 0, t:t + 1],
                          scalar2=None, op0=ALU.is_equal)
                    I("dve", "tensor_scalar", ["iotab", kk_], [("P1", pb)] + xk, out=P1[pb][:, q, :], in0=iotab[:], scalar1=kgT[:, 1, t:t + 1],
                      scalar2=kgT[:, 2, t:t + 1], op0=ALU.is_equal, op1=ALU.mult)

            def w_apply(tg):
                pb = tg % 2
                bank = 6 + pb
                for q in range(4):
                    MM(PS[bank][:, q * 128:(q + 1) * 128], P0[pb][:, q, :], P1[pb][:, q, :], True, True, [("P0", pb, q), ("P1", pb)], [PK[bank]])
                I("dve", "tensor_tensor", [PK[bank]] + ABK, ABK, out=AB[:, :, tg * 4:(tg + 1) * 4],
                  in0=PS[bank][:, :].rearrange("p (t k) -> p k t", t=4), in1=AB[:, :, tg * 4:(tg + 1) * 4], op=ALU.mult)
            return w_build, w_apply

        def V_phase(bb, after=None):
            AB = ABufs[bb % 2]
            ABK = ABKs[bb % 2]
            t0 = bb * TB
            DMA("sp", hb2, h1_v[:, :, t0:t0 + TB], ["h1s"], ["hb2"])
            ringv = [(vb[i][:, 0, :], f"vb{i}", ([("P0", i, q) for q in range(4)] + [("P1", i)]) if i < 2 else []) for i in range(NVB)] + \
                    [(ub[NUB - 1][:].rearrange("p a b -> p (a b)"), f"ub{NUB - 1}", [])]

            def ldv(i, bf):
                DMA("sp", ringv[bf][0], VRb_v[:, i, :], [("VRb", i // 8)], [ringv[bf][1]] + ringv[bf][2])

            def usev(k1, bf):
                for dq in range(8):
                    MM(PS[dq // 2][:, (dq % 2) * TB:(dq % 2 + 1) * TB], ringv[bf][0][:, dq * 128:(dq + 1) * 128], AB[:, k1, :],
                       k1 == 0, k1 == 127, [ringv[bf][1]] + ABK, [PK[dq // 2]])

            def aft(i):
                if after is not None and i % 2 == 1:
                    after(i // 2)
            stream(128, len(ringv), ldv, usev, after=aft)
            for dq in range(8):
                tk = ("tmpf", dq % 2)
                tmp = tmpA if dq % 2 == 0 else tmpB
                ACT(tmp, PS[dq // 2][:, (dq % 2) * TB:(dq % 2 + 1) * TB], AF.Copy, [PK[dq // 2], "modT"], [tk], scale=g2[:, dq:dq + 1])
                I("pool", "tensor_tensor", [tk, "hb2"], ["hb2"], out=hb2[:, dq, :], in0=hb2[:, dq, :], in1=tmp, op=ALU.add)

        def final_steps(bb):
            t0 = bb * TB

            def fin_stats():
                for dc in range(8):
                    ACT(tmpA, hb2[:, dc, :], AF.Square, ["hb2"], [("tmpf", 0)])
                    MM(PS[3][:, 0:TB], ones, tmpA, dc == 0, dc == 7, [("tmpf", 0), "cst"], [PK[3]])

            def fin_rstd():
                rstd_from(PS[3][:, 0:TB], TB, 1.0 / D, [PK[3]])

            def fin_out():
                for dc in range(8):
                    I("dve", "scalar_tensor_tensor", ["hb2", "vecs", "rstd"], ["hb2"], out=hb2[:, dc, :], in0=hb2[:, dc, :],
                      scalar=fng[:, dc:dc + 1], in1=rstd[:, 0:TB], op0=ALU.mult, op1=ALU.mult)
                DMA("sp", out_v[:, :, t0:t0 + TB], hb2, ["hb2"], ["outT"])
            return [fin_stats, fin_rstd, fin_out]

        run_routing_now(routing_steps(0))
        if nblk2 > 1:
            run_routing_now(routing_steps(1))
        A_phase(0)
        pend_final = None
        spill = None
        for b2 in range(nblk2):
            w_build, w_apply = W_fns(b2)
            sched1 = {}
            if pend_final is not None:
                sched1.setdefault(6, []).append(pend_final[0])
                sched1.setdefault(14, []).append(pend_final[1])
                sched1.setdefault(22, []).append(pend_final[2])
            if spill is not None:
                sched1.setdefault(50, []).append(spill)
            w_build(0)

            def after_a(i):
                for f in sched1.pop(i, []):
                    f()
                if i % 2 == 1:
                    g = i // 2
                    if g + 1 < NG:
                        w_build(g + 1)
                    w_apply(g)
            if b2 + 1 < nblk2:
                A_phase(b2 + 1, after=after_a)
            else:
                for i in range(128):
                    after_a(i)
            for i in sorted(sched1):
                for f in sched1[i]:
                    f()
            RN = routing_steps(b2 + 2) if b2 + 2 < nblk2 else None
            vsched = {}
            if RN is not None:
                def put(i, f):
                    vsched.setdefault(i, []).append(f)
                for j, f in enumerate(RN["pre"]):
                    put(j // 2, f)
                for j, f in enumerate(RN["q"][0]):
                    put(4 + j, f)
                put(20, RN["sc"][0])
                put(20, RN["ch"][0])
                for j, f in enumerate(RN["q"][1]):
                    put(22 + j, f)
                put(38, RN["sc"][1])
                put(38, RN["ch"][1])
                put(57, RN["tr"][0])
                spill = RN["tr_spill"]
            else:
                spill = None
            vcount = [0]

            def after_v(i):
                for f in vsched.pop(vcount[0], []):
                    f()
                vcount[0] += 1
            V_phase(b2, after=after_v)
            for i in sorted(vsched):
                for f in vsched[i]:
                    f()
            pend_final = final_steps(b2)
        for f in pend_final:
            f()
        S.wait_all("sp", ["outT", "h1s"])
        S.emit(st)
    return nc


def _consts():
    c = np.zeros((128, NC), np.float32)
    i = np.arange(128)
    c[:, 0:128] = np.eye(128, dtype=np.float32)
    c[:, 128:256] = i[None, :]
    same = (i[:, None] // 64) == (i[None, :] // 64)
    c[:, 256:384] = (same & (i[:, None] <= i[None, :])).astype(np.float32)
    c[:, 384:512] = same.astype(np.float32)
    c[:, 512:640] = (i[:, None] < 64).astype(np.float32) * np.ones((1, 128), np.float32)
    c[:, 640:768] = (i[:, None] >= 64).astype(np.float32) * np.ones((1, 128), np.float32)
    c[:, 768:896] = 1.0
    c[:, 896:912] = np.arange(16)[None, :]
    return c


def _fm(v, nch):
    return np.ascontiguousarray(np.asarray(v, np.float32).reshape(nch, 128).T)


def prep_shared(inp):
    f = lambda k: np.asarray(inp[k], np.float32)
    vecs = np.zeros((128, NV), np.float32)
    vecs[:, 0:8] = _fm(f("norm1_g")[0], 8)
    vecs[:, 8:16] = _fm(f("norm2_g")[0], 8)
    vecs[:, 16:24] = _fm(f("final_norm_g"), 8)
    vecs[:, 24:72] = _fm(f("ada_b")[0], 48)
    cw = f("ssd_conv_w")[0]
    for k in range(4):
        vecs[:, 72 + k * 8:72 + (k + 1) * 8] = _fm(cw[k], 8)
    vecs[:, 104:112] = _fm(f("ssd_conv_b")[0], 8)
    vecs[:, 112:116] = _fm(f("ssd_norm_g")[0], 4)
    dw = f("conf_dw_w")[0]
    for j in range(4):
        for k in range(31):
            vecs[:, 116 + j * 31 + k] = dw[k, j * 128:(j + 1) * 128]
    vecs[:, 240:244] = _fm(f("conf_dw_b")[0], 4)
    vecs[:, 244:248] = _fm(f("conf_ln_g")[0], 4)
    vecs[:, 248:252] = _fm(f("conf_ln_b")[0], 4)
    hp = np.zeros((128, 536), np.float32)
    hp[:, 0:8] = f("ssd_dt_bias")[0][None, :]
    hp[:, 8:16] = f("ssd_a_log")[0][None, :]
    hp[:, 24:536] = np.repeat(f("ssd_d")[0], 64)[None, :]
    keys = f("peer_sub_keys")[0]
    keysT = np.ascontiguousarray(keys.transpose(3, 0, 1, 2).reshape(128, 2048))
    U = f("peer_u")[0].reshape(128, 128, 8, 128)
    UT = np.ascontiguousarray(U.transpose(1, 3, 2, 0)).reshape(128, 128, 1024)
    V = f("peer_v")[0].reshape(128, 128, 1024)
    VR = np.ascontiguousarray(V.transpose(1, 0, 2))
    wq = f("peer_w_query")[0].reshape(8, 128, 16, 128)
    WQ = np.ascontiguousarray(wq.transpose(2, 1, 0, 3)).reshape(16, 128, 1024)
    return dict(ada_w=np.ascontiguousarray(f("ada_w")[0]), vecs=vecs, hp=hp,
                w_in=np.ascontiguousarray(f("w_in")[0]), w_out=np.ascontiguousarray(f("w_out")[0]),
                w_q=WQ, keysT=keysT, UT=UT, VR=VR, cst=_consts())


def prep_core(inp, b, s_tok):
    x = np.asarray(inp["x"], np.float32)[b, :s_tok]
    c = np.asarray(inp["c"], np.float32)[b]
    return dict(xT=np.ascontiguousarray(x.T), cT=_fm(c, 8))


def kernel(**inputs):
    B, S_TOK = 8, 4096
    nc = build(S_TOK)
    shared = prep_shared(inputs)
    in_maps = []
    for b in range(B):
        m = dict(shared)
        m.update(prep_core(inputs, b, S_TOK))
        in_maps.append(m)
    res = run_bass_kernel_spmd(nc, in_maps, core_ids=list(range(B)))
    out = np.stack([np.ascontiguousarray(np.asarray(r["outT"], np.float32).T) for r in res.results], axis=0)
    return out
```

```python
import numpy as np
from contextlib import ExitStack
import concourse.bass as bass
import concourse.mybir as mybir
from concourse.bass_utils import run_bass_kernel_spmd

F32 = mybir.dt.float32
BF16 = mybir.dt.bfloat16
U32 = mybir.dt.uint32
AF = mybir.ActivationFunctionType
ALU = mybir.AluOpType
AX = mybir.AxisListType

class Sched:
    ENG = ("pe", "dve", "act", "pool", "sp")
    EPOCH = 30000
    NDMA = 24

    def __init__(self, nc):
        self.nc = nc
        self.ops = {e: [] for e in self.ENG}
        self.res = {}
        self.dma_rr = {e: 0 for e in self.ENG}
        self.dma_last = {e: [None] * self.NDMA for e in self.ENG}

    def issue(self, eng, fn, reads=(), writes=(), dma=False):
        deps = set()
        for k in reads:
            r = self.res.get(k)
            if r is not None and r["w"] is not None:
                deps.add((r["w"], "raw"))
        for k in writes:
            r = self.res.get(k)
            if r is not None:
                if r["w"] is not None:
                    deps.add((r["w"], "waw"))
                for x in r["r"]:
                    deps.add((x, "war"))
        idx = len(self.ops[eng])
        me = (eng, idx)
        slot = None
        if dma:
            slot = self.dma_rr[eng] % self.NDMA
            self.dma_rr[eng] += 1
            prev = self.dma_last[eng][slot]
            if prev is not None:
                deps.add((prev, "raw"))
            self.dma_last[eng][slot] = me
        keep = set()
        for (p, kind) in deps:
            pe_, pi_ = p
            prod = self.ops[pe_][pi_]
            if (not prod["dma"]) and (not dma) and pe_ == eng:
                if eng == "pe":
                    continue
            keep.add(p)
        self.ops[eng].append(dict(fn=fn, deps=keep, dma=dma, slot=slot, inc=False))
        for k in reads:
            r = self.res.setdefault(k, dict(w=None, r=[]))
            r["r"].append(me)
        for k in writes:
            self.res[k] = dict(w=me, r=[])
        return me

    def barrier(self):
        lasts = set()
        for e in self.ENG:
            for i in range(len(self.ops[e]) - 1, -1, -1):
                op = self.ops[e][i]
                if op["fn"] is not None and not op["dma"]:
                    lasts.add((e, i))
                    break
            for prev in self.dma_last[e]:
                if prev is not None:
                    lasts.add(prev)
        for e in self.ENG:
            self.ops[e].append(dict(fn=None, deps=set(lasts), dma=False, slot=None, inc=False))

    def wait_all(self, eng, keys):
        return self.issue(eng, None, reads=list(keys), writes=())

    def emit(self, stack):
        nc = self.nc
        for e in self.ENG:
            for op in self.ops[e]:
                for (pe_, pi_) in op["deps"]:
                    self.ops[pe_][pi_]["inc"] = True
        nsem_eng = {}
        for e in self.ENG:
            cnt = 0
            dcnt = [0] * self.NDMA
            for op in self.ops[e]:
                if op["dma"]:
                    dcnt[op["slot"]] += 16
                    op["sem"] = ("d", e, op["slot"])
                    op["val"] = dcnt[op["slot"]]
                elif op["inc"]:
                    ep = cnt // self.EPOCH
                    cnt += 1
                    op["sem"] = ("c", e, ep)
                    op["val"] = cnt - ep * self.EPOCH
            nsem_eng[e] = cnt // self.EPOCH + 1
        sems = {}
        for e in self.ENG:
            if any((not op["dma"]) and op["inc"] for op in self.ops[e]):
                for ep in range(nsem_eng[e]):
                    sems[("c", e, ep)] = stack.enter_context(nc.semaphore(f"c_{e}_{ep}"))
            used = set(op["slot"] for op in self.ops[e] if op["dma"])
            for s in used:
                sems[("d", e, s)] = stack.enter_context(nc.semaphore(f"d_{e}_{s}"))
        ops = self.ops

        def run(e, engine):
            seen = {}
            for op in ops[e]:
                need = {}
                for (pe_, pi_) in op["deps"]:
                    p = ops[pe_][pi_]
                    s, v = p["sem"], p["val"]
                    if need.get(s, 0) < v:
                        need[s] = v
                for s, v in need.items():
                    if seen.get(s, 0) < v:
                        engine.wait_ge(sems[s], v)
                        seen[s] = v
                if op["fn"] is None:
                    continue
                inst = op["fn"](engine)
                if op["dma"]:
                    inst.then_inc(sems[op["sem"]], 16)
                elif op["inc"]:
                    inst.then_inc(sems[op["sem"]], 1)

        with nc.Block() as block:
            @block.tensor
            def _(eng):
                run("pe", eng)

            @block.vector
            def _(eng):
                run("dve", eng)

            @block.scalar
            def _(eng):
                run("act", eng)

            @block.gpsimd
            def _(eng):
                run("pool", eng)

            @block.sync
            def _(eng):
                run("sp", eng)


D = 1024
NV = 252
NC = 912
EPS = 1e-6
NEG = -1e30


def build(S_TOK, dbg=False):
    nc = bass.Bass("TRN2", target_bir_lowering=False)
    NB1 = 256
    TB = 256
    nblk1 = S_TOK // NB1
    nblk2 = S_TOK // TB

    def din(name, shape, dt=F32):
        return nc.dram_tensor(name, shape, dt, kind="ExternalInput").ap()

    xT = din("xT", [D, S_TOK])
    cT = din("cT", [128, 8])
    ada_w = din("ada_w", [D, 6144])
    vecs_d = din("vecs", [128, NV])
    hp_d = din("hp", [128, 536])
    w_in = din("w_in", [D, 2568])
    w_out = din("w_out", [D, D])
    w_q = din("w_q", [16, 128, 1024])
    keysT = din("keysT", [128, 2048])
    UT = din("UT", [128, 128, 1024])
    VR = din("VR", [128, 128, 1024])
    cst_d = din("cst", [128, NC])
    UTb = nc.dram_tensor("UTb", [128, 128, 1024], BF16, kind="Internal").ap()
    VRb = nc.dram_tensor("VRb", [128, 128, 1024], BF16, kind="Internal").ap()
    WQb = nc.dram_tensor("WQb", [16, 128, 1024], BF16, kind="Internal").ap()
    if dbg:
        h1s = nc.dram_tensor("h1s", [D, S_TOK], F32, kind="ExternalOutput").ap()
    else:
        h1s = nc.dram_tensor("h1s", [D, S_TOK], F32, kind="Internal").ap()
    outT = nc.dram_tensor("outT", [D, S_TOK], F32, kind="ExternalOutput").ap()

    with ExitStack() as st:
        S = Sched(nc)

        def sb(n, s, d=F32):
            return st.enter_context(nc.sbuf_tensor(n, s, d))

        def I(eng, name, reads, writes, *a, **kw):
            return S.issue(eng, lambda e: getattr(e, name)(*a, **kw), reads, writes)

        def DMA(eng, out, in_, reads, writes):
            return S.issue(eng, lambda e: e.dma_start(out=out, in_=in_), reads, writes, dma=True)

        def MM(out, lhsT, rhs, start, stop, reads, writes):
            return S.issue("pe", lambda e: e.matmul(out, lhsT=lhsT, rhs=rhs, start=start, stop=stop), reads, writes)

        def TR(out, in_, ident, reads, writes):
            return S.issue("pe", lambda e: e.transpose(out=out, in_=in_, identity=ident), reads, writes)

        def ACT(out, in_, func, reads, writes, **kw):
            return S.issue("act", lambda e: e.activation(out=out, in_=in_, func=func, **kw), reads, writes)

        PS = [st.enter_context(nc.psum_tensor(f"ps{i}", [128, 512], F32)) for i in range(8)]
        PK = [f"ps{i}" for i in range(8)]

        NA32 = 51300
        arena = sb("arena", [128, NA32])
        aptr = {1: 0, 2: 0}

        def carve(which, shape, dt=F32):
            n = 1
            for q in shape[1:]:
                n *= q
            esz = 4 if dt in (F32, U32) else 2
            n32 = (n * esz + 3) // 4
            off = aptr[which]
            aptr[which] = off + n32
            assert aptr[which] <= NA32, (which, aptr[which])
            v = arena[:, off:off + n32]
            if dt != F32:
                v = v.bitcast(dt)[:, 0:n]
            if len(shape) == 3:
                v = v.rearrange("p (a b) -> p a b", a=shape[1])
            elif len(shape) == 4:
                v = v.rearrange("p (a b c) -> p a b c", a=shape[1], b=shape[2])
            return v

        pers = sb("pers", [128, 1900])
        pptr = [0]

        def PS_(name, shape, dt=F32):
            n = 1
            for q in shape[1:]:
                n *= q
            esz = 4 if dt in (F32, U32) else 2
            n32 = (n * esz + 3) // 4
            off = pptr[0]
            pptr[0] = off + n32
            assert pptr[0] <= 1900
            v = pers[:, off:off + n32]
            if dt != F32:
                v = v.bitcast(dt)[:, 0:n]
            return T(v, name)

        class T:
            def __init__(self, v, name):
                self.v = v
                self.name = name
            def __getitem__(self, k):
                return self.v[k]

        def A1(name, shape, dt=F32):
            return T(carve(1, shape, dt), name)

        def A2(name, shape, dt=F32):
            return T(carve(2, shape, dt), name)

        cst = PS_("cst_s", [128, 400])
        vecs = PS_("vecs_s", [128, NV])
        DMA("sp", cst[:, 0:256], cst_d[:, 0:256], [], ["cst"])
        DMA("sp", cst[:, 256:400], cst_d[:, 768:912], [], ["cst"])
        DMA("sp", vecs[:], vecs_d, [], ["vecs"])
        ident = cst[:, 0:128]
        iota_f = cst[:, 128:256]
        ones = cst[:, 256:384]
        iota16 = cst[:, 384:400]
        identb = PS_("identb", [128, 128], BF16)
        iotab = PS_("iotab", [128, 128], BF16)
        I("dve", "tensor_copy", ["cst"], ["identb"], out=identb[:], in_=ident)
        I("dve", "tensor_copy", ["cst"], ["iotab"], out=iotab[:], in_=iota_f)
        n1g, n2g, fng = vecs[:, 0:8], vecs[:, 8:16], vecs[:, 16:24]
        adab = vecs[:, 24:72]

        aptr[3] = 23400
        adw = [carve(3, [128, 8, 768]) for i in range(2)]
        ada_v = ada_w.rearrange("(dc p) f -> p dc f", p=128)
        cTs = PS_("cTs", [128, 8])
        cond = PS_("cond", [128, 8])
        DMA("sp", cTs[:], cT, [], ["cTs"])
        ACT(cond[:], cTs[:], AF.Silu, ["cTs"], ["cond"])
        modT = PS_("modT", [128, 48])
        for pc in range(8):
            b = pc % 2
            DMA("sp", adw[b], ada_v[:, :, pc * 768:(pc + 1) * 768], [], [("adw", b)])
            for f6 in range(6):
                fc = pc * 6 + f6
                for dc in range(8):
                    MM(PS[0][:, fc:fc + 1], adw[b][:, dc, f6 * 128:(f6 + 1) * 128], cond[:, dc:dc + 1],
                       dc == 0, dc == 7, [("adw", b), "cond"], [PK[0]])
        I("dve", "tensor_tensor", [PK[0], "vecs"], ["modT"], out=modT[:], in0=PS[0][:, 0:48], in1=adab, op=ALU.add)
        sh1, sc1, g1 = modT[:, 0:8], modT[:, 8:16], modT[:, 16:24]
        sh2, sc2, g2 = modT[:, 24:32], modT[:, 32:40], modT[:, 40:48]
        gm = PS_("gm", [128, 16])
        I("dve", "tensor_scalar", ["modT"], ["gm"], out=gm[:, 0:8], in0=sc1, scalar1=1.0, scalar2=None, op0=ALU.add)
        I("dve", "tensor_scalar", ["modT"], ["gm"], out=gm[:, 8:16], in0=sc2, scalar1=1.0, scalar2=None, op0=ALU.add)
        I("dve", "tensor_tensor", ["gm", "vecs"], ["gm"], out=gm[:, 0:8], in0=gm[:, 0:8], in1=n1g, op=ALU.mult)
        I("dve", "tensor_tensor", ["gm", "vecs"], ["gm"], out=gm[:, 8:16], in0=gm[:, 8:16], in1=n2g, op=ALU.mult)
        gm1, gm2 = gm[:, 0:8], gm[:, 8:16]

        winb = A1("winb", [128, 8, 2568], BF16)
        woutb = A1("woutb", [128, 8, 1024], BF16)
        dg = A1("dg", [128, 124, 128], BF16)
        cst1 = A1("cst1", [128, 512])
        hp = A1("hp", [128, 536])
        assert aptr[1] <= 23400
        DMA("sp", cst1[:], cst_d[:, 256:768], [], ["cst"])
        DMA("sp", hp[:], hp_d, [], ["hp"])
        tri = cst1[:, 0:128]
        blk = cst1[:, 128:256]
        chsel = [cst1[:, 256:384], cst1[:, 384:512]]
        stg_big = [carve(3, [128, 2568]) for i in range(2)]
        w_in_v = w_in.rearrange("(dc p) n -> p dc n", p=128)
        w_out_v = w_out.rearrange("(dc p) n -> p dc n", p=128)
        ci = 0
        for (src_v, dst, n) in ((w_in_v, winb, 2568), (w_out_v, woutb, 1024)):
            for dc in range(8):
                b = ci % 2
                DMA("act", stg_big[b][:, 0:n], src_v[:, dc, :], [], [("stg", b)])
                if ci % 2 == 0:
                    I("dve", "tensor_copy", [("stg", b)], [dst.name], out=dst[:, dc, :], in_=stg_big[b][:, 0:n])
                else:
                    ACT(dst[:, dc, :], stg_big[b][:, 0:n], AF.Copy, [("stg", b)], [dst.name])
                ci += 1
        for j in range(124):
            eng = "dve"
            I(eng, "tensor_scalar", ["cst", "vecs"], ["dg"], out=dg[:, j, :], in0=ident,
              scalar1=vecs[:, 116 + j:117 + j], scalar2=None, op0=ALU.mult)

        UT2 = UT.rearrange("a b c -> (a b) c").rearrange("(r q) c -> r (q c)", q=8)
        UTb2 = UTb.rearrange("a b c -> (a b) c").rearrange("(r q) c -> r (q c)", q=8)
        VR2 = VR.rearrange("a b c -> (a b) c").rearrange("(r q) c -> r (q c)", q=8)
        VRb2 = VRb.rearrange("a b c -> (a b) c").rearrange("(r q) c -> r (q c)", q=8)
        DMA("pool", WQb.rearrange("a b c -> (a b) c"), w_q.rearrange("a b c -> (a b) c"), [], ["WQb"])
        cast_jobs = []
        for i in range(16):
            cast_jobs.append((UTb2[i * 128:(i + 1) * 128, :], UT2[i * 128:(i + 1) * 128, :], ("UTb", i)))
            cast_jobs.append((VRb2[i * 128:(i + 1) * 128, :], VR2[i * 128:(i + 1) * 128, :], ("VRb", i)))

        aneg = PS_("aneg", [128, 8])
        ACT(aneg[:], hp[:, 8:16], AF.Exp, ["hp"], ["aneg"])
        I("dve", "tensor_scalar", ["aneg"], ["aneg"], out=aneg[:], in0=aneg[:], scalar1=-1.0, scalar2=None, op0=ALU.mult)
        dskip = hp[:, 24:536]

        S.barrier()
        tmpf_t = PS_("tmpf_t", [128, 512])
        tmpf = None
        rstd = PS_("rstd", [128, 512])
        rstd_full = rstd
        tmpf = [T(tmpf_t[:, 0:NB1], "tmpf"), T(tmpf_t[:, NB1:2 * NB1], "tmpf")]
        rstd = T(rstd_full[:, 0:NB1], "rstd")
        acc = tmpf
        xt = [A1(f"xt{i}", [128, 8, NB1]) for i in range(3)]
        hns = [A1(f"hn{i}", [128, 8, NB1], BF16) for i in range(2)]
        zss = [A1(f"zs{i}", [128, 4, NB1]) for i in range(2)]
        xr = A1("xr", [128, 8, 3 + NB1])
        xbcss = [A1(f"xbcs{i}", [128, 8, NB1], BF16) for i in range(2)]
        uin = A1("uin", [128, 4, 30 + NB1], BF16)
        ucv = A1("ucv", [128, 4, NB1])
        ycats = [A1(f"ycat{i}", [128, 8, NB1], BF16) for i in range(2)]
        yg = A1("yg", [128, 4, NB1])
        tmpfB = [A1(f"tmpfB{i}", [128, NB1]) for i in range(2)]
        rstdB = rstd_full[:, NB1:2 * NB1]
        mean = A1("mean", [128, NB1])
        Hf = A1("Hf", [128, 512])
        Hb = [A1(f"Hb{i}", [128, 512], BF16) for i in range(2)]
        sm = A1("sm", [128, 96])
        acs_sb = A1("acs_sb", [128, 8])
        decb = A1("decb", [128, 16])
        arg = A1("arg", [128, 8, 128])
        ex = A1("ex", [128, 8, 128])
        gmk = A1("gmk", [128, 2, 128])
        scb = A1("scb", [128, 8, 128], BF16)
        tok = A1("tok", [128, 768], BF16)
        xw = A1("xw", [128, 512], BF16)
        xd = A1("xd", [128, 512], BF16)
        ytmp = A1("ytmp", [128, 512])

        I("dve", "memset", [], ["xr"], xr[:], 0.0)
        I("dve", "memset", [], ["uin"], uin[:], 0.0)
        I("dve", "memset", [], ["Hf"], Hf[:], 0.0)
        I("dve", "memset", [], ["Hb0"], Hb[0][:], 0.0)

        xT_v = xT.rearrange("(dc p) t -> p dc t", p=128)
        h1_v = h1s.rearrange("(dc p) t -> p dc t", p=128)
        out_v = outT.rearrange("(dc p) t -> p dc t", p=128)

        def rstd_from(ps_ap, n, scale, reads_extra=(), dst=None, key="rstd"):
            d_ = rstd[:, 0:n] if dst is None else dst
            I("dve", "tensor_scalar", list(reads_extra), [key], out=d_, in0=ps_ap, scalar1=scale, scalar2=EPS,
              op0=ALU.mult, op1=ALU.add)
            ACT(d_, d_, AF.Ln, [key], [key])
            ACT(d_, d_, AF.Exp, [key], [key], scale=-0.5)

        DMA("sp", xt[0][:], xT_v[:, :, 0:NB1], [], ["xt0"])
        ncj = (32 + nblk1 - 1) // nblk1

        def stageA(blkI):
            pr = blkI % 2
            X = xt[blkI % 3]
            XK = f"xt{blkI % 3}"
            t0 = blkI * NB1
            hn = hns[pr]; HNK = f"hn{pr}"
            zs = zss[pr]; ZK = f"zs{pr}"
            xbcs = xbcss[pr]
            ycat = ycats[pr]
            if blkI + 1 < nblk1:
                DMA("sp", xt[(blkI + 1) % 3][:], xT_v[:, :, t0 + NB1:t0 + 2 * NB1], [], [f"xt{(blkI + 1) % 3}"])
            for cj in cast_jobs[blkI * ncj:(blkI + 1) * ncj]:
                DMA("pool", cj[0], cj[1], [], [cj[2]])
            yield
            for dc in range(8):
                tb = dc % 2
                ACT(tmpf[tb][:], X[:, dc, :], AF.Square, [XK], [("tmpf", tb)])
                MM(PS[0][:, 0:NB1], ones, tmpf[tb][:], dc == 0, dc == 7, [("tmpf", tb), "cst"], [PK[0]])
            rstd_from(PS[0][:, 0:NB1], NB1, 1.0 / D, [PK[0]])
            yield
            for dc in range(8):
                tb = dc % 2
                I("dve", "scalar_tensor_tensor", [XK, "gm", "rstd"], [("tmpf", tb)], out=tmpf[tb][:], in0=X[:, dc, :],
                  scalar=gm1[:, dc:dc + 1], in1=rstd[:, :], op0=ALU.mult, op1=ALU.mult)
                ACT(hn[:, dc, :], tmpf[tb][:], AF.Identity, [("tmpf", tb), "modT"], [HNK], bias=sh1[:, dc:dc + 1])
                if dc == 3:
                    yield
            yield

            def proj(col0, ncols, bank):
                for dc in range(8):
                    MM(PS[bank][0:ncols, 0:NB1], winb[:, dc, col0:col0 + ncols], hn[:, dc, :], dc == 0, dc == 7,
                       ["winb", HNK], [PK[bank]])

            for j in range(4):
                bank = j % 2
                proj(j * 128, 128, bank)
                ACT(zs[:, j, :], PS[bank][:, 0:NB1], AF.Silu, [PK[bank]], [ZK])
                if j % 2 == 1:
                    yield
            for j in range(8):
                bank = j % 2
                if blkI > 0:
                    I("dve", "tensor_copy", [("xr", j)], [("xr", j)], out=xr[:, j, 0:3], in_=xr[:, j, NB1:NB1 + 3])
                proj(512 + j * 128, 128, bank)
                ACT(xr[:, j, 3:3 + NB1], PS[bank][:, 0:NB1], AF.Copy, [PK[bank]], [("xr", j)])
                a = acc[j % 2]
                ak = ("tmpf", j % 2)
                I("dve", "tensor_scalar", [("xr", j), "vecs"], [ak], out=a[:], in0=xr[:, j, 3:3 + NB1],
                  scalar1=vecs[:, 72 + 3 * 8 + j:72 + 3 * 8 + j + 1], scalar2=None, op0=ALU.mult)
                for k in range(3):
                    I("dve", "scalar_tensor_tensor", [("xr", j), "vecs", ak], [ak], out=a[:], in0=xr[:, j, k:k + NB1],
                      scalar=vecs[:, 72 + k * 8 + j:72 + k * 8 + j + 1], in1=a[:], op0=ALU.mult, op1=ALU.add)
                ACT(xbcs[:, j, :], a[:], AF.Silu, [ak, "vecs"], [("xbcs", pr, j)], bias=vecs[:, 104 + j:105 + j])
                yield
            for j in range(4):
                if blkI > 0:
                    I("dve", "tensor_copy", [("uin", j)], [("uin", j)], out=uin[:, j, 0:30], in_=uin[:, j, NB1:NB1 + 30])
                proj(2056 + j * 128, 128, 0)
                proj(1544 + j * 128, 128, 1)
                tb = j % 2
                ACT(tmpf[tb][:], PS[0][:, 0:NB1], AF.Sigmoid, [PK[0]], [("tmpf", tb)])
                I("dve", "tensor_tensor", [PK[1], ("tmpf", tb)], [("uin", j)], out=uin[:, j, 30:30 + NB1], in0=PS[1][:, 0:NB1],
                  in1=tmpf[tb][:], op=ALU.mult)
                yield
                for k in range(31):
                    MM(PS[0][:, 0:NB1], dg[:, j * 31 + k, :], uin[:, j, k:k + NB1], k == 0, k == 30, ["dg", ("uin", j)], [PK[0]])
                    if k == 15:
                        pass
                ACT(ucv[:, j, :], PS[0][:, 0:NB1], AF.Identity, [PK[0], "vecs"], [("ucv", j)], bias=vecs[:, 240 + j:241 + j])
                yield
            for j in range(4):
                MM(PS[0][:, 0:NB1], ones, ucv[:, j, :], j == 0, j == 3, [("ucv", j), "cst"], [PK[0]])
            for j in range(4):
                tb = j % 2
                ACT(tmpf[tb][:], ucv[:, j, :], AF.Square, [("ucv", j)], [("tmpf", tb)])
                MM(PS[1][:, 0:NB1], ones, tmpf[tb][:], j == 0, j == 3, [("tmpf", tb), "cst"], [PK[1]])
            I("dve", "tensor_scalar", [PK[0]], ["mean"], out=mean[:], in0=PS[0][:, 0:NB1], scalar1=1.0 / 512, scalar2=None, op0=ALU.mult)
            I("dve", "tensor_tensor", ["mean"], [("tmpf", 0)], out=tmpf[0][:], in0=mean[:], in1=mean[:], op=ALU.mult)
            I("dve", "scalar_tensor_tensor", [PK[1], ("tmpf", 0)], [("tmpf", 1)], out=tmpf[1][:], in0=PS[1][:, 0:NB1], scalar=1.0 / 512,
              in1=tmpf[0][:], op0=ALU.mult, op1=ALU.subtract)
            rstd_from(tmpf[1][:], NB1, 1.0, [("tmpf", 1)])
            yield
            for j in range(4):
                tb = j % 2
                I("dve", "tensor_tensor", [("ucv", j), "mean"], [("tmpf", tb)], out=tmpf[tb][:], in0=ucv[:, j, :], in1=mean[:], op=ALU.subtract)
                I("dve", "tensor_tensor", [("tmpf", tb), "rstd"], [("tmpf", tb)], out=tmpf[tb][:], in0=tmpf[tb][:], in1=rstd[:, :], op=ALU.mult)
                ACT(ycat[:, 4 + j, :], tmpf[tb][:], AF.Silu, [("tmpf", tb), "vecs"], [("ycat", pr, 4 + j)],
                    scale=vecs[:, 244 + j:245 + j], bias=vecs[:, 248 + j:249 + j])
                if j == 1:
                    yield
            yield

        def stageB(blkI):
            pr = blkI % 2
            X = xt[blkI % 3]
            XK = f"xt{blkI % 3}"
            t0 = blkI * NB1
            hn = hns[pr]; HNK = f"hn{pr}"
            zs = zss[pr]; ZK = f"zs{pr}"
            xbcs = xbcss[pr]
            ycat = ycats[pr]
            XB = lambda j: ("xbcs", pr, j)
            YGK = ["yg"]
            for sbI in range(NB1 // 128):
                T0 = sbI * 128
                TS = slice(T0, T0 + 128)
                for dc in range(8):
                    MM(PS[6][:, 0:8], hn[:, dc, TS], winb[:, dc, 1536:1544], dc == 0, dc == 7, [HNK, "winb"], [PK[6]])
                xb_ = sm[:, 0:8]; ax = sm[:, 8:16]; ee = sm[:, 16:24]; dtv = sm[:, 24:32]; dA = sm[:, 32:40]
                eacs = sm[:, 40:48]; toend = sm[:, 48:56]; tt_ = sm[:, 56:64]
                I("dve", "tensor_tensor", [PK[6], "hp"], ["sm"], out=xb_, in0=PS[6][:, 0:8], in1=hp[:, 0:8], op=ALU.add)
                ACT(ee, xb_, AF.Exp, ["sm"], ["sm"])
                ACT(dtv, ee, AF.Ln, ["sm", "cst"], ["sm"], bias=ones[:, 0:1])
                yield
                I("dve", "tensor_tensor", ["sm", "aneg"], ["sm"], out=dA, in0=dtv, in1=aneg[:], op=ALU.mult)
                MM(PS[6][:, 8:16], tri, dA, True, True, ["sm", "cst"], [PK[6]])
                MM(PS[6][:, 16:24], blk, dA, True, True, ["sm", "cst"], [PK[6]])
                MM(PS[6][:, 24:32], chsel[0], dA, True, True, ["sm", "cst"], [PK[6]])
                MM(PS[6][:, 32:40], chsel[1], dA, True, True, ["sm", "cst"], [PK[6]])
                for h in range(8):
                    bank = 4 + h // 4
                    MM(PS[bank][:, (h % 4) * 128:(h % 4 + 1) * 128], dA[:, h:h + 1].to_broadcast([128, 128]), tri, True, True,
                       ["sm", "cst"], [PK[bank]])
                yield
                I("dve", "tensor_copy", [PK[6]], ["acs_sb"], out=acs_sb[:], in_=PS[6][:, 8:16])
                ACT(eacs, PS[6][:, 8:16], AF.Exp, [PK[6]], ["sm2"])
                I("dve", "tensor_tensor", [PK[6], "acs_sb"], ["sm2"], out=tt_, in0=PS[6][:, 16:24], in1=acs_sb[:], op=ALU.subtract)
                ACT(tt_, tt_, AF.Exp, ["sm2"], ["sm2"])
                I("dve", "tensor_tensor", ["sm2", "sm"], ["sm2"], out=toend, in0=tt_, in1=dtv, op=ALU.mult)
                ACT(decb[:], PS[6][:, 24:40], AF.Exp, [PK[6]], ["decb"])
                yield
                for hh in range(2):
                    bank = 4 + hh
                    I("dve", "tensor_tensor", [PK[bank], "acs_sb"], [("arg", hh)], out=arg[:, hh * 4:(hh + 1) * 4, :],
                      in0=PS[bank][:, :].rearrange("p (h t) -> p h t", h=4),
                      in1=acs_sb[:, hh * 4:(hh + 1) * 4].unsqueeze(2).to_broadcast([128, 4, 128]), op=ALU.subtract)
                    I("dve", "tensor_scalar", [("arg", hh)], [("arg", hh)], out=arg[:, hh * 4:(hh + 1) * 4, :],
                      in0=arg[:, hh * 4:(hh + 1) * 4, :], scalar1=0.0, scalar2=None, op0=ALU.min)
                    ACT(ex[:, hh * 4:(hh + 1) * 4, :], arg[:, hh * 4:(hh + 1) * 4, :], AF.Exp, [("arg", hh)], [("ex", hh)])
                for g in range(2):
                    MM(PS[6][:, 128 + g * 128:256 + g * 128], xbcs[:, 4 + g, TS], xbcs[:, 6 + g, TS], True, True,
                       [XB(4 + g), XB(6 + g)], [PK[6]])
                yield
                I("dve", "tensor_tensor", [PK[6], "cst"], ["gmk"], out=gmk[:], in0=PS[6][:, 128:384].rearrange("p (g t) -> p g t", g=2),
                  in1=tri.unsqueeze(1).to_broadcast([128, 2, 128]), op=ALU.mult)
                for g in range(2):
                    I("dve", "tensor_tensor", [("ex", g), "gmk"], [("ex", g)], out=ex[:, g * 4:(g + 1) * 4, :], in0=ex[:, g * 4:(g + 1) * 4, :],
                      in1=gmk[:, g:g + 1, :].to_broadcast([128, 4, 128]), op=ALU.mult)
                I("dve", "tensor_tensor", [("ex", 0), ("ex", 1), "sm"], ["scb"], out=scb[:], in0=ex[:],
                  in1=dtv.unsqueeze(2).to_broadcast([128, 8, 128]), op=ALU.mult)
                trp = PS[3][:, :].bitcast(BF16)
                for j in range(6):
                    TR(trp[:, j * 128:(j + 1) * 128], xbcs[:, j, TS], identb[:], [XB(j), "identb"], [PK[3]])
                ACT(tok[:], trp[:, 0:768], AF.Copy, [PK[3]], ["tok"])
                yield
                I("dve", "tensor_tensor", ["tok", "sm2"], ["xw"], out=xw[:].rearrange("p (h q) -> p h q", h=8),
                  in0=tok[:, 0:512].rearrange("p (h q) -> p h q", h=8), in1=toend.unsqueeze(2).to_broadcast([128, 8, 64]), op=ALU.mult)
                I("pool", "tensor_tensor", ["tok", "hp"], ["xd"], out=xd[:], in0=tok[:, 0:512], in1=dskip, op=ALU.mult)
                for h in range(8):
                    hs = slice(h * 64, (h + 1) * 64)
                    MM(PS[2][:, hs], scb[:, h, :], tok[:, hs], True, False, ["scb", "tok"], [PK[2]])
                    MM(PS[2][:, hs], identb[:], xd[:, hs], False, True, ["identb", "xd"], [PK[2]])
                yield
                for c in range(2):
                    R = slice(c * 64, (c + 1) * 64)
                    hb_in = Hb[c]
                    hb_out = Hb[(c + 1) % 2]
                    for g in range(2):
                        gs = slice(g * 256, (g + 1) * 256)
                        MM(PS[7][R, gs], xbcs[:, 6 + g, T0 + c * 64:T0 + (c + 1) * 64], hb_in[:, gs], True, True,
                           [XB(6 + g), f"Hb{c}"], [PK[7]])
                        MM(PS[3][:, gs], tok[R, 512 + g * 128:512 + (g + 1) * 128], xw[R, gs], True, True, ["tok", "xw"], [PK[3]])
                    I("dve", "tensor_tensor", ["Hf", "decb"], ["Hf"], out=Hf[:].rearrange("p (h q) -> p h q", h=8),
                      in0=Hf[:].rearrange("p (h q) -> p h q", h=8),
                      in1=decb[:, c * 8:(c + 1) * 8].unsqueeze(2).to_broadcast([128, 8, 64]), op=ALU.mult)
                    I("dve", "tensor_tensor", ["Hf", PK[3]], ["Hf"], out=Hf[:], in0=Hf[:], in1=PS[3][:, :], op=ALU.add)
                    ACT(hb_out[:], Hf[:], AF.Copy, ["Hf"], [f"Hb{(c + 1) % 2}"])
                    I("dve", "tensor_tensor", [PK[7], "sm2"], ["ytmp"], out=ytmp[R, :].rearrange("p (h q) -> p h q", h=8),
                      in0=PS[7][R, :].rearrange("p (h q) -> p h q", h=8), in1=eacs[R, :].unsqueeze(2).to_broadcast([64, 8, 64]), op=ALU.mult)
                    yield
                I("dve", "tensor_tensor", ["ytmp", PK[2]], ["ytmp"], out=ytmp[:], in0=ytmp[:], in1=PS[2][:, :], op=ALU.add)
                for j in range(4):
                    TR(PS[2][:, j * 128:(j + 1) * 128], ytmp[:, j * 128:(j + 1) * 128], ident, ["ytmp", "cst"], [PK[2]])
                I("dve", "tensor_tensor", [PK[2], ZK], YGK, out=yg[:, :, TS], in0=PS[2][:, :].rearrange("p (j t) -> p j t", j=4),
                  in1=zs[:, :, TS], op=ALU.mult)
                yield
            for j in range(4):
                tb = j % 2
                ACT(tmpfB[tb][:], yg[:, j, :], AF.Square, YGK, [("tmpfB", tb)])
                MM(PS[7][:, 0:NB1], ones, tmpfB[tb][:], j == 0, j == 3, [("tmpfB", tb), "cst"], [PK[7]])
            rstd_from(PS[7][:, 0:NB1], NB1, 1.0 / 512, [PK[7]], dst=rstdB, key="rstdB")
            yield
            for j in range(4):
                I("dve", "scalar_tensor_tensor", YGK + ["vecs", "rstdB"], [("ycat", pr, j)], out=ycat[:, j, :], in0=yg[:, j, :],
                  scalar=vecs[:, 112 + j:113 + j], in1=rstdB, op0=ALU.mult, op1=ALU.mult)
            yield
            ROTB = (2, 3, 4, 5)
            for dcn in range(8):
                bank = ROTB[dcn % 4]
                for kc in range(8):
                    MM(PS[bank][:, 0:NB1], woutb[:, kc, dcn * 128:(dcn + 1) * 128], ycat[:, kc, :], kc == 0, kc == 7,
                       ["woutb"] + [("ycat", pr, q) for q in range(8)], [PK[bank]])
                I("dve", "scalar_tensor_tensor", [PK[bank], "modT", XK], [XK], out=X[:, dcn, :], in0=PS[bank][:, 0:NB1],
                  scalar=g1[:, dcn:dcn + 1], in1=X[:, dcn, :], op0=ALU.mult, op1=ALU.add)
                if dcn % 2 == 1:
                    yield
            DMA("sp", h1_v[:, :, t0:t0 + NB1], X[:], [XK], ["h1s"])
            yield

        for _ in stageA(0):
            pass
        for blkI in range(nblk1):
            gB = stageB(blkI)
            gA = stageA(blkI + 1) if blkI + 1 < nblk1 else None
            doneA = gA is None
            doneB = False
            while not (doneA and doneB):
                if not doneB:
                    try:
                        next(gB)
                    except StopIteration:
                        doneB = True
                if not doneA:
                    try:
                        next(gA)
                    except StopIteration:
                        doneA = True

        S.barrier()
        rstd = rstd_full
        TB = 256
        nblk2 = S_TOK // TB
        NT = TB // 128
        NG = TB // 4
        ABufs = [carve(2, [128, 128, TB], BF16) for _ in range(2)]
        hn2s = [A2(f"hn2{i}", [128, 8, TB], BF16) for i in range(2)]
        keysb = A2("keysb", [128, 2048], BF16)
        DMA("pool", keysb[:], keysT, [], ["keysb"])
        kgTs = [A2(f"kgT{i}", [128, 3, TB]) for i in range(2)]
        NUB, NVB = 4, 6
        ub = [A2(f"ub{i}", [128, 8, 128], BF16) for i in range(NUB)]
        vb = [A2(f"vb{i}", [128, 1, 1024], BF16) for i in range(NVB)]
        P0 = [T(vb[i][:, 0, 0:512].rearrange("p (a b) -> p a b", a=4), "P0") for i in range(2)]
        P1 = [T(vb[i][:, 0, 512:1024].rearrange("p (a b) -> p a b", a=4), "P1") for i in range(2)]
        aab = [A2(f"aab{i}", [128, 128], BF16) for i in range(2)]
        hb2 = carve(2, [128, 8, TB])
        rs0 = aptr[2]
        sc = A2("sc", [128, 16, 128])
        hbv = arena[:, rs0:rs0 + 2048].rearrange("p (a b) -> p a b", a=8)
        cand = A2("cand", [128, 4, 256])
        qT = A2("qT", [128, 16, 128], BF16)
        m16 = A2("m16", [128, 16, 16])
        i16 = A2("i16", [128, 16, 16], U32)
        i16f = A2("i16f", [128, 16, 16])
        b16 = A2("b16", [128, 8, 16])
        p16 = A2("p16", [128, 8, 16], U32)
        pab = A2("pab", [128, 2, 128], U32)
        pabf = A2("pabf", [128, 2, 128])
        kgf2 = [A2(f"kgf{i}", [128, 3, 128]) for i in range(2)]
        zz = A2("zz", [128, 16])
        eq = T(sc[:].rearrange("p a k -> p (a k)").rearrange("p (x y) -> p x y", y=16), "eq")
        tmpA = tmpf_t[:, 0:256]
        tmpB = tmpf_t[:, 256:512]
        SCK = [("sc", q) for q in range(16)]
        CANDK = [("cand", h) for h in range(4)]
        M16K = [("m16", q, v) for q in range(16) for v in range(2)]
        I16K = [("i16", q, v) for q in range(16) for v in range(2)]
        B16K = [("b16", h, v) for h in range(8) for v in range(2)]
        P16K = [("p16", h, v) for h in range(8) for v in range(2)]
        HBK = SCK
        ABKs = [[("AB", i, q) for q in range(8)] for i in range(2)]
        UTb_v = UTb.rearrange("k p (dc q) -> p k dc q", dc=8)
        VRb_v = VRb.rearrange("k q d -> q k d")
        CERF = 1.0

        def stream(n, nbuf, load_fn, use_fn, after=None):
            for i in range(min(nbuf - 1, n)):
                load_fn(i, i % nbuf)
            for i in range(n):
                if i + nbuf - 1 < n:
                    load_fn(i + nbuf - 1, (i + nbuf - 1) % nbuf)
                use_fn(i, i % nbuf)
                if after is not None:
                    after(i)

        def routing_steps(bb):
            t0 = bb * TB
            hn2 = hn2s[bb % 2]
            hk = f"hn2{bb % 2}"
            kgT = kgTs[bb % 2]
            kk_ = f"kgT{bb % 2}"
            steps = []
            steps.append(lambda: DMA("sp", hbv, h1_v[:, :, t0:t0 + TB], ["h1s"], HBK))

            def st_stats(lo, hi):
                def f():
                    for dc in range(lo, hi):
                        ACT(tmpA, hbv[:, dc, :], AF.Square, HBK, [("tmpf", 0)])
                        MM(PS[4][:, 0:TB], ones, tmpA, dc == 0, dc == 7, [("tmpf", 0), "cst"], [PK[4]])
                return f
            steps.append(st_stats(0, 4))
            steps.append(st_stats(4, 8))
            steps.append(lambda: rstd_from(PS[4][:, 0:TB], TB, 1.0 / D, [PK[4]]))

            def st_hn(lo, hi):
                def f():
                    for dc in range(lo, hi):
                        I("dve", "scalar_tensor_tensor", HBK + ["gm", "rstd"], [("tmpf", 0)], out=tmpA, in0=hbv[:, dc, :],
                          scalar=gm2[:, dc:dc + 1], in1=rstd[:, 0:TB], op0=ALU.mult, op1=ALU.mult)
                        ACT(hn2[:, dc, :], tmpA, AF.Identity, [("tmpf", 0), "modT"], [hk], bias=sh2[:, dc:dc + 1])
                return f
            steps.append(st_hn(0, 4))
            steps.append(st_hn(4, 8))

            def q_steps(tt):
                TS = slice(tt * 128, (tt + 1) * 128)
                out = []

                NQ = NUB - 1

                def ldq(i):
                    bf = i % NQ
                    DMA("act", ub[bf][:], WQb[i].rearrange("p (dc c) -> p dc c", dc=8), ["WQb"], [f"ub{bf}"])

                def mk(i):
                    def f():
                        if i == 0:
                            for j in range(NQ - 1):
                                ldq(j)
                        if i + NQ - 1 < 16:
                            ldq(i + NQ - 1)
                        bf = i % NQ
                        bank = 4 + (i // 4) % 2
                        for dc in range(8):
                            MM(PS[bank][:, (i % 4) * 128:(i % 4 + 1) * 128], ub[bf][:, dc, :], hn2[:, dc, TS], dc == 0, dc == 7,
                               [f"ub{bf}", hk], [PK[bank]])
                        if i % 4 == 3:
                            g4 = i // 4
                            ACT(qT[:, g4 * 4:(g4 + 1) * 4, :], PS[bank][:, :].rearrange("p (a t) -> p a t", a=4), AF.Copy, [PK[bank]], ["qT"])
                    return f
                for i in range(16):
                    out.append(mk(i))
                return out

            def scores_step(tt):
                def f():
                    for half in range(2):
                        for q8 in range(8):
                            qc = half * 8 + q8
                            bank = 6 + q8 // 4
                            MM(PS[bank][:, (q8 % 4) * 128:(q8 % 4 + 1) * 128], qT[:, qc, :], keysb[:, qc * 128:(qc + 1) * 128], True, True,
                               ["qT", "keysb"], [PK[bank]])
                        for bb_ in range(2):
                            q0 = half * 8 + bb_ * 4
                            ACT(sc[:, q0:q0 + 4, :], PS[6 + bb_][:, :].rearrange("p (a k) -> p a k", a=4), AF.Copy, [PK[6 + bb_]],
                                [("sc", q0 + q) for q in range(4)])
                return f

            def chain_step(tt):
                kgf = kgf2[tt % 2]
                kp = tt % 2

                def f():
                    for qc in range(16):
                        I("dve", "max", [("sc", qc)], [("m16", qc, 0)], out=m16[:, qc, 0:8], in_=sc[:, qc, :])
                    for qc in range(16):
                        I("dve", "max_index", [("sc", qc), ("m16", qc, 0)], [("i16", qc, 0)], out=i16[:, qc, 0:8], in_max=m16[:, qc, 0:8], in_values=sc[:, qc, :])
                    for qc in range(16):
                        I("dve", "match_replace", [("sc", qc), ("m16", qc, 0)], [("sc", qc)], out=sc[:, qc, :], in_to_replace=m16[:, qc, 0:8], in_values=sc[:, qc, :], imm_value=NEG)
                    for qc in range(16):
                        I("dve", "max", [("sc", qc)], [("m16", qc, 1)], out=m16[:, qc, 8:16], in_=sc[:, qc, :])
                    for qc in range(16):
                        I("dve", "max_index", [("sc", qc), ("m16", qc, 1)], [("i16", qc, 1)], out=i16[:, qc, 8:16], in_max=m16[:, qc, 8:16], in_values=sc[:, qc, :])
                    I("dve", "tensor_copy", I16K, ["i16f"], out=i16f[:], in_=i16[:])
                    m16v = m16[:].rearrange("p (h i) a -> p h i a", i=2)
                    i16v = i16f[:].rearrange("p (h i) a -> p h i a", i=2)
                    for hh in range(2):
                        H4 = slice(hh * 4, (hh + 1) * 4)
                        I("dve", "tensor_tensor", M16K, CANDK, out=cand[:].rearrange("p h (a b) -> p h a b", a=16),
                          in0=m16v[:, H4, 0, :].unsqueeze(3).to_broadcast([128, 4, 16, 16]),
                          in1=m16v[:, H4, 1, :].unsqueeze(2).to_broadcast([128, 4, 16, 16]), op=ALU.add)
                        for h4 in range(4):
                            h = hh * 4 + h4
                            I("dve", "max", [("cand", h4)], [("b16", h, 0)], out=b16[:, h, 0:8], in_=cand[:, h4, :])
                        for h4 in range(4):
                            h = hh * 4 + h4
                            I("dve", "max_index", [("cand", h4), ("b16", h, 0)], [("p16", h, 0)], out=p16[:, h, 0:8], in_max=b16[:, h, 0:8], in_values=cand[:, h4, :])
                        for h4 in range(4):
                            h = hh * 4 + h4
                            I("dve", "match_replace", [("cand", h4), ("b16", h, 0)], [("cand", h4)], out=cand[:, h4, :], in_to_replace=b16[:, h, 0:8], in_values=cand[:, h4, :], imm_value=NEG)
                        for h4 in range(4):
                            h = hh * 4 + h4
                            I("dve", "max", [("cand", h4)], [("b16", h, 1)], out=b16[:, h, 8:16], in_=cand[:, h4, :])
                        for h4 in range(4):
                            h = hh * 4 + h4
                            I("dve", "max_index", [("cand", h4), ("b16", h, 1)], [("p16", h, 1)], out=p16[:, h, 8:16], in_max=b16[:, h, 8:16], in_values=cand[:, h4, :])
                    p16f = p16[:].rearrange("p h j -> p (h j)")
                    I("dve", "tensor_single_scalar", P16K, ["pab"], out=pab[:, 0, :], in_=p16f, scalar=4, op=ALU.logical_shift_right)
                    I("dve", "tensor_single_scalar", P16K, ["pab"], out=pab[:, 1, :], in_=p16f, scalar=15, op=ALU.bitwise_and)
                    I("dve", "tensor_copy", ["pab"], ["pabf"], out=pabf[:], in_=pab[:])
                    for i2 in range(2):
                        I("dve", "tensor_tensor", ["pabf", "cst"], SCK, out=eq[:].rearrange("p (h j) a -> p h j a", h=8),
                          in0=iota16.unsqueeze(1).unsqueeze(1).to_broadcast([128, 8, 16, 16]),
                          in1=pabf[:, i2, :].rearrange("p (h j) -> p h j", h=8).unsqueeze(3).to_broadcast([128, 8, 16, 16]), op=ALU.is_equal)
                        I("dve", "tensor_tensor", SCK + ["i16f"], SCK, out=eq[:].rearrange("p (h j) a -> p h j a", h=8),
                          in0=eq[:].rearrange("p (h j) a -> p h j a", h=8),
                          in1=i16v[:, :, i2, :].unsqueeze(2).to_broadcast([128, 8, 16, 16]), op=ALU.mult)
                        I("dve", "tensor_reduce", SCK, [("kgf", kp, i2)], out=kgf[:, i2, :], in_=eq[:], axis=AX.X, op=ALU.add)
                    gv = kgf[:, 2, :].rearrange("p (h j) -> p h j", h=8)
                    I("dve", "tensor_tensor", B16K, [("kgf", kp, 2)], out=gv, in0=b16[:], in1=b16[:, :, 0:1].to_broadcast([128, 8, 16]), op=ALU.subtract)
                    ACT(gv, gv, AF.Exp, [("kgf", kp, 2)], [("kgf", kp, 2)])
                    I("dve", "tensor_reduce", [("kgf", kp, 2)], ["zz"], out=zz[:, 0:8], in_=gv, axis=AX.X, op=ALU.add)
                    I("dve", "tensor_scalar", ["zz"], ["zz"], out=zz[:, 0:8], in0=zz[:, 0:8], scalar1=CERF, scalar2=None, op0=ALU.mult)
                    I("dve", "reciprocal", ["zz"], ["zz"], out=zz[:, 8:16], in_=zz[:, 0:8])
                    I("dve", "tensor_tensor", [("kgf", kp, 2), "zz"], [("kgf", kp, 2)], out=gv, in0=gv, in1=zz[:, 8:16].unsqueeze(2).to_broadcast([128, 8, 16]), op=ALU.mult)
                return f

            def tr_step(tt, bank):
                TS = slice(tt * 128, (tt + 1) * 128)
                kgf = kgf2[tt % 2]
                kp = tt % 2

                def f():
                    KG = [("kgf", kp, q) for q in range(3)]
                    for q in range(3):
                        TR(PS[bank][:, q * 128:(q + 1) * 128], kgf[:, q, :], ident, KG + ["cst"], [PK[bank]])
                    ACT(kgT[:, :, TS], PS[bank][:, 0:384].rearrange("p (q t) -> p q t", q=3), AF.Copy, [PK[bank]], [kk_])
                return f

            return dict(pre=steps, q=[q_steps(tt) for tt in range(NT)], sc=[scores_step(tt) for tt in range(NT)],
                        ch=[chain_step(tt) for tt in range(NT)], tr=[tr_step(tt, 6) for tt in range(NT)], tr_spill=tr_step(NT - 1, 2))

        def run_routing_now(R):
            for f in R["pre"]:
                f()
            for tt in range(NT):
                for f in R["q"][tt]:
                    f()
                R["sc"][tt]()
                R["ch"][tt]()
                R["tr"][tt]()

        def A_phase(bb, after=None):
            AB = ABufs[bb % 2]
            hn2 = hn2s[bb % 2]
            hk = f"hn2{bb % 2}"

            ring = [(ub[i][:], f"ub{i}") for i in range(NUB)] + \
                   [(vb[i][:].rearrange("p a (dc c) -> p (a dc) c", c=128), f"vb{i}") for i in range(2, NVB)]

            def ldu(i, bf):
                DMA("sp", ring[bf][0], UTb_v[:, i, :, :], [("UTb", i // 8)], [ring[bf][1]])

            def useu(k1, bf):
                bank = 4 + k1 % 2
                for dc in range(8):
                    MM(PS[bank][:, 0:TB], ring[bf][0][:, dc, :], hn2[:, dc, :], dc == 0, dc == 7, [ring[bf][1], hk], [PK[bank]])
                ACT(AB[:, k1, :], PS[bank][:, 0:TB], AF.Gelu, [PK[bank]], [("AB", bb % 2, k1 // 16)])
            stream(128, len(ring), ldu, useu, after=after)

        def W_fns(bb):
            AB = ABufs[bb % 2]
            ABK = ABKs[bb % 2]
            kgT = kgTs[bb % 2]
            kk_ = f"kgT{bb % 2}"

            def w_build(tg):
                pb = tg % 2
                xk = [f"vb{pb}"] if tg < 2 else []
                for q in range(4):
                    t = tg * 4 + q
                    if q == 0:
                        ACT(aab[pb][:], iotab[:], AF.Abs, ["iotab", kk_], [("aab", pb)], scale=-1.0, bias=kgT[:, 0, t:t + 1])
                        ACT(P0[pb][:, q, :], aab[pb][:], AF.Relu, [("aab", pb), "cst"], [("P0", pb, q)] + xk, scale=-1.0, bias=ones[:, 0:1])
                    else:
                        I("dve", "tensor_scalar", ["iotab", kk_], [("P0", pb, q)] + xk, out=P0[pb][:, q, :], in0=iotab[:], scalar1=kgT[:, 0, t:t + 1],
                          scalar2=None, op0=ALU.is_equal)
                    I("dve", "tensor_scalar", ["iotab", kk_], [("P1", pb)] + xk, out=P1[pb][:, q, :], in0=iotab[:], scalar1=kgT[:, 1, t:t + 1],
                      scalar2=kgT[:, 2, t:t + 1], op0=ALU.is_equal, op1=ALU.mult)

            def w_apply(tg):
                pb = tg % 2
                bank = 6 + pb
                for q in range(4):
                    MM(PS[bank][:, q * 128:(q + 1) * 128], P0[pb][:, q, :], P1[pb][:, q, :], True, True, [("P0", pb, q), ("P1", pb)], [PK[bank]])
                I("dve", "tensor_tensor", [PK[bank]] + ABK, ABK, out=AB[:, :, tg * 4:(tg + 1) * 4],
                  in0=PS[bank][:, :].rearrange("p (t k) -> p k t", t=4), in1=AB[:, :, tg * 4:(tg + 1) * 4], op=ALU.mult)
            return w_build, w_apply

        def V_phase(bb, after=None):
            AB = ABufs[bb % 2]
            ABK = ABKs[bb % 2]
            t0 = bb * TB
            DMA("sp", hb2, h1_v[:, :, t0:t0 + TB], ["h1s"], ["hb2"])
            ringv = [(vb[i][:, 0, :], f"vb{i}", ([("P0", i, q) for q in range(4)] + [("P1", i)]) if i < 2 else []) for i in range(NVB)] + \
                    [(ub[NUB - 1][:].rearrange("p a b -> p (a b)"), f"ub{NUB - 1}", [])]

            def ldv(i, bf):
                DMA("sp", ringv[bf][0], VRb_v[:, i, :], [("VRb", i // 8)], [ringv[bf][1]] + ringv[bf][2])

            def usev(k1, bf):
                for dq in range(8):
                    MM(PS[dq // 2][:, (dq % 2) * TB:(dq % 2 + 1) * TB], ringv[bf][0][:, dq * 128:(dq + 1) * 128], AB[:, k1, :],
                       k1 == 0, k1 == 127, [ringv[bf][1]] + ABK, [PK[dq // 2]])

            def aft(i):
                if after is not None and i % 2 == 1:
                    after(i // 2)
            stream(128, len(ringv), ldv, usev, after=aft)
            for dq in range(8):
                tk = ("tmpf", dq % 2)
                tmp = tmpA if dq % 2 == 0 else tmpB
                ACT(tmp, PS[dq // 2][:, (dq % 2) * TB:(dq % 2 + 1) * TB], AF.Copy, [PK[dq // 2], "modT"], [tk], scale=g2[:, dq:dq + 1])
                I("pool", "tensor_tensor", [tk, "hb2"], ["hb2"], out=hb2[:, dq, :], in0=hb2[:, dq, :], in1=tmp, op=ALU.add)

        def final_steps(bb):
            t0 = bb * TB

            def fin_stats():
                for dc in range(8):
                    ACT(tmpA, hb2[:, dc, :], AF.Square, ["hb2"], [("tmpf", 0)])
                    MM(PS[3][:, 0:TB], ones, tmpA, dc == 0, dc == 7, [("tmpf", 0), "cst"], [PK[3]])

            def fin_rstd():
                rstd_from(PS[3][:, 0:TB], TB, 1.0 / D, [PK[3]])

            def fin_out():
                for dc in range(8):
                    I("dve", "scalar_tensor_tensor", ["hb2", "vecs", "rstd"], ["hb2"], out=hb2[:, dc, :], in0=hb2[:, dc, :],
                      scalar=fng[:, dc:dc + 1], in1=rstd[:, 0:TB], op0=ALU.mult, op1=ALU.mult)
                DMA("sp", out_v[:, :, t0:t0 + TB], hb2, ["hb2"], ["outT"])
            return [fin_stats, fin_rstd, fin_out]

        run_routing_now(routing_steps(0))
        if nblk2 > 1:
            run_routing_now(routing_steps(1))
        A_phase(0)
        pend_final = None
        spill = None
        for b2 in range(nblk2):
            w_build, w_apply = W_fns(b2)
            sched1 = {}
            if pend_final is not None:
                sched1.setdefault(6, []).append(pend_final[0])
                sched1.setdefault(14, []).append(pend_final[1])
                sched1.setdefault(22, []).append(pend_final[2])
            if spill is not None:
                sched1.setdefault(50, []).append(spill)
            w_build(0)

            def after_a(i):
                for f in sched1.pop(i, []):
                    f()
                if i % 2 == 1:
                    g = i // 2
                    if g + 1 < NG:
                        w_build(g + 1)
                    w_apply(g)
            if b2 + 1 < nblk2:
                A_phase(b2 + 1, after=after_a)
            else:
                for i in range(128):
                    after_a(i)
            for i in sorted(sched1):
                for f in sched1[i]:
                    f()
            RN = routing_steps(b2 + 2) if b2 + 2 < nblk2 else None
            vsched = {}
            if RN is not None:
                def put(i, f):
                    vsched.setdefault(i, []).append(f)
                for j, f in enumerate(RN["pre"]):
                    put(j // 2, f)
                for j, f in enumerate(RN["q"][0]):
                    put(4 + j, f)
                put(20, RN["sc"][0])
                put(20, RN["ch"][0])
                for j, f in enumerate(RN["q"][1]):
                    put(22 + j, f)
                put(38, RN["sc"][1])
                put(38, RN["ch"][1])
                put(57, RN["tr"][0])
                spill = RN["tr_spill"]
            else:
                spill = None
            vcount = [0]

            def after_v(i):
                for f in vsched.pop(vcount[0], []):
                    f()
                vcount[0] += 1
            V_phase(b2, after=after_v)
            for i in sorted(vsched):
                for f in vsched[i]:
                    f()
            pend_final = final_steps(b2)
        for f in pend_final:
            f()
        S.wait_all("sp", ["outT", "h1s"])
        S.emit(st)
    return nc


def _consts():
    c = np.zeros((128, NC), np.float32)
    i = np.arange(128)
    c[:, 0:128] = np.eye(128, dtype=np.float32)
    c[:, 128:256] = i[None, :]
    same = (i[:, None] // 64) == (i[None, :] // 64)
    c[:, 256:384] = (same & (i[:, None] <= i[None, :])).astype(np.float32)
    c[:, 384:512] = same.astype(np.float32)
    c[:, 512:640] = (i[:, None] < 64).astype(np.float32) * np.ones((1, 128), np.float32)
    c[:, 640:768] = (i[:, None] >= 64).astype(np.float32) * np.ones((1, 128), np.float32)
    c[:, 768:896] = 1.0
    c[:, 896:912] = np.arange(16)[None, :]
    return c


def _fm(v, nch):
    return np.ascontiguousarray(np.asarray(v, np.float32).reshape(nch, 128).T)


def prep_shared(inp):
    f = lambda k: np.asarray(inp[k], np.float32)
    vecs = np.zeros((128, NV), np.float32)
    vecs[:, 0:8] = _fm(f("norm1_g")[0], 8)
    vecs[:, 8:16] = _fm(f("norm2_g")[0], 8)
    vecs[:, 16:24] = _fm(f("final_norm_g"), 8)
    vecs[:, 24:72] = _fm(f("ada_b")[0], 48)
    cw = f("ssd_conv_w")[0]
    for k in range(4):
        vecs[:, 72 + k * 8:72 + (k + 1) * 8] = _fm(cw[k], 8)
    vecs[:, 104:112] = _fm(f("ssd_conv_b")[0], 8)
    vecs[:, 112:116] = _fm(f("ssd_norm_g")[0], 4)
    dw = f("conf_dw_w")[0]
    for j in range(4):
        for k in range(31):
            vecs[:, 116 + j * 31 + k] = dw[k, j * 128:(j + 1) * 128]
    vecs[:, 240:244] = _fm(f("conf_dw_b")[0], 4)
    vecs[:, 244:248] = _fm(f("conf_ln_g")[0], 4)
    vecs[:, 248:252] = _fm(f("conf_ln_b")[0], 4)
    hp = np.zeros((128, 536), np.float32)
    hp[:, 0:8] = f("ssd_dt_bias")[0][None, :]
    hp[:, 8:16] = f("ssd_a_log")[0][None, :]
    hp[:, 24:536] = np.repeat(f("ssd_d")[0], 64)[None, :]
    keys = f("peer_sub_keys")[0]
    keysT = np.ascontiguousarray(keys.transpose(3, 0, 1, 2).reshape(128, 2048))
    U = f("peer_u")[0].reshape(128, 128, 8, 128)
    UT = np.ascontiguousarray(U.transpose(1, 3, 2, 0)).reshape(128, 128, 1024)
    V = f("peer_v")[0].reshape(128, 128, 1024)
    VR = np.ascontiguousarray(V.transpose(1, 0, 2))
    wq = f("peer_w_query")[0].reshape(8, 128, 16, 128)
    WQ = np.ascontiguousarray(wq.transpose(2, 1, 0, 3)).reshape(16, 128, 1024)
    return dict(ada_w=np.ascontiguousarray(f("ada_w")[0]), vecs=vecs, hp=hp,
                w_in=np.ascontiguousarray(f("w_in")[0]), w_out=np.ascontiguousarray(f("w_out")[0]),
                w_q=WQ, keysT=keysT, UT=UT, VR=VR, cst=_consts())


def prep_core(inp, b, s_tok):
    x = np.asarray(inp["x"], np.float32)[b, :s_tok]
    c = np.asarray(inp["c"], np.float32)[b]
    return dict(xT=np.ascontiguousarray(x.T), cT=_fm(c, 8))


def kernel(**inputs):
    B, S_TOK = 8, 4096
    nc = build(S_TOK)
    shared = prep_shared(inputs)
    in_maps = []
    for b in range(B):
        m = dict(shared)
        m.update(prep_core(inputs, b, S_TOK))
        in_maps.append(m)
    res = run_bass_kernel_spmd(nc, in_maps, core_ids=list(range(B)))
    out = np.stack([np.ascontiguousarray(np.asarray(r["outT"], np.float32).T) for r in res.results], axis=0)
    return out
```

```python
import numpy as np
from contextlib import ExitStack
import concourse.bass as bass
import concourse.mybir as mybir
from concourse.bass_utils import run_bass_kernel_spmd

F32 = mybir.dt.float32
BF16 = mybir.dt.bfloat16
U32 = mybir.dt.uint32
AF = mybir.ActivationFunctionType
ALU = mybir.AluOpType
AX = mybir.AxisListType

class Sched:
    ENG = ("pe", "dve", "act", "pool", "sp")
    EPOCH = 30000
    NDMA = 24

    def __init__(self, nc):
        self.nc = nc
        self.ops = {e: [] for e in self.ENG}
        self.res = {}
        self.dma_rr = {e: 0 for e in self.ENG}
        self.dma_last = {e: [None] * self.NDMA for e in self.ENG}

    def issue(self, eng, fn, reads=(), writes=(), dma=False):
        deps = set()
        for k in reads:
            r = self.res.get(k)
            if r is not None and r["w"] is not None:
                deps.add((r["w"], "raw"))
        for k in writes:
            r = self.res.get(k)
            if r is not None:
                if r["w"] is not None:
                    deps.add((r["w"], "waw"))
                for x in r["r"]:
                    deps.add((x, "war"))
        idx = len(self.ops[eng])
        me = (eng, idx)
        slot = None
        if dma:
            slot = self.dma_rr[eng] % self.NDMA
            self.dma_rr[eng] += 1
            prev = self.dma_last[eng][slot]
            if prev is not None:
                deps.add((prev, "raw"))
            self.dma_last[eng][slot] = me
        keep = set()
        for (p, kind) in deps:
            pe_, pi_ = p
            prod = self.ops[pe_][pi_]
            if (not prod["dma"]) and (not dma) and pe_ == eng:
                if eng == "pe":
                    continue
            keep.add(p)
        self.ops[eng].append(dict(fn=fn, deps=keep, dma=dma, slot=slot, inc=False))
        for k in reads:
            r = self.res.setdefault(k, dict(w=None, r=[]))
            r["r"].append(me)
        for k in writes:
            self.res[k] = dict(w=me, r=[])
        return me

    def barrier(self):
        lasts = set()
        for e in self.ENG:
            for i in range(len(self.ops[e]) - 1, -1, -1):
                op = self.ops[e][i]
                if op["fn"] is not None and not op["dma"]:
                    lasts.add((e, i))
                    break
            for prev in self.dma_last[e]:
                if prev is not None:
                    lasts.add(prev)
        for e in self.ENG:
            self.ops[e].append(dict(fn=None, deps=set(lasts), dma=False, slot=None, inc=False))

    def wait_all(self, eng, keys):
        return self.issue(eng, None, reads=list(keys), writes=())

    def emit(self, stack):
        nc = self.nc
        for e in self.ENG:
            for op in self.ops[e]:
                for (pe_, pi_) in op["deps"]:
                    self.ops[pe_][pi_]["inc"] = True
        nsem_eng = {}
        for e in self.ENG:
            cnt = 0
            dcnt = [0] * self.NDMA
            for op in self.ops[e]:
                if op["dma"]:
                    dcnt[op["slot"]] += 16
                    op["sem"] = ("d", e, op["slot"])
                    op["val"] = dcnt[op["slot"]]
                elif op["inc"]:
                    ep = cnt // self.EPOCH
                    cnt += 1
                    op["sem"] = ("c", e, ep)
                    op["val"] = cnt - ep * self.EPOCH
            nsem_eng[e] = cnt // self.EPOCH + 1
        sems = {}
        for e in self.ENG:
            if any((not op["dma"]) and op["inc"] for op in self.ops[e]):
                for ep in range(nsem_eng[e]):
                    sems[("c", e, ep)] = stack.enter_context(nc.semaphore(f"c_{e}_{ep}"))
            used = set(op["slot"] for op in self.ops[e] if op["dma"])
            for s in used:
                sems[("d", e, s)] = stack.enter_context(nc.semaphore(f"d_{e}_{s}"))
        ops = self.ops

        def run(e, engine):
            seen = {}
            for op in ops[e]:
                need = {}
                for (pe_, pi_) in op["deps"]:
                    p = ops[pe_][pi_]
                    s, v = p["sem"], p["val"]
                    if need.get(s, 0) < v:
                        need[s] = v
                for s, v in need.items():
                    if seen.get(s, 0) < v:
                        engine.wait_ge(sems[s], v)
                        seen[s] = v
                if op["fn"] is None:
                    continue
                inst = op["fn"](engine)
                if op["dma"]:
                    inst.then_inc(sems[op["sem"]], 16)
                elif op["inc"]:
                    inst.then_inc(sems[op["sem"]], 1)

        with nc.Block() as block:
            @block.tensor
            def _(eng):
                run("pe", eng)

            @block.vector
            def _(eng):
                run("dve", eng)

            @block.scalar
            def _(eng):
                run("act", eng)

            @block.gpsimd
            def _(eng):
                run("pool", eng)

            @block.sync
            def _(eng):
                run("sp", eng)


D = 1024
NV = 252
NC = 912
EPS = 1e-6
NEG = -1e30


def build(S_TOK, dbg=False):
    nc = bass.Bass("TRN2", target_bir_lowering=False)
    NB1 = 256
    TB = 256
    nblk1 = S_TOK // NB1
    nblk2 = S_TOK // TB

    def din(name, shape, dt=F32):
        return nc.dram_tensor(name, shape, dt, kind="ExternalInput").ap()

    xT = din("xT", [D, S_TOK])
    cT = din("cT", [128, 8])
    ada_w = din("ada_w", [D, 6144])
    vecs_d = din("vecs", [128, NV])
    hp_d = din("hp", [128, 536])
    w_in = din("w_in", [D, 2568])
    w_out = din("w_out", [D, D])
    w_q = din("w_q", [16, 128, 1024])
    keysT = din("keysT", [128, 2048])
    UT = din("UT", [128, 128, 1024])
    VR = din("VR", [128, 128, 1024])
    cst_d = din("cst", [128, NC])
    UTb = nc.dram_tensor("UTb", [128, 128, 1024], BF16, kind="Internal").ap()
    VRb = nc.dram_tensor("VRb", [128, 128, 1024], BF16, kind="Internal").ap()
    WQb = nc.dram_tensor("WQb", [16, 128, 1024], BF16, kind="Internal").ap()
    if dbg:
        h1s = nc.dram_tensor("h1s", [D, S_TOK], F32, kind="ExternalOutput").ap()
    else:
        h1s = nc.dram_tensor("h1s", [D, S_TOK], F32, kind="Internal").ap()
    outT = nc.dram_tensor("outT", [D, S_TOK], F32, kind="ExternalOutput").ap()

    with ExitStack() as st:
        S = Sched(nc)

        def sb(n, s, d=F32):
            return st.enter_context(nc.sbuf_tensor(n, s, d))

        def I(eng, name, reads, writes, *a, **kw):
            return S.issue(eng, lambda e: getattr(e, name)(*a, **kw), reads, writes)

        def DMA(eng, out, in_, reads, writes):
            return S.issue(eng, lambda e: e.dma_start(out=out, in_=in_), reads, writes, dma=True)

        def MM(out, lhsT, rhs, start, stop, reads, writes):
            return S.issue("pe", lambda e: e.matmul(out, lhsT=lhsT, rhs=rhs, start=start, stop=stop), reads, writes)

        def TR(out, in_, ident, reads, writes):
            return S.issue("pe", lambda e: e.transpose(out=out, in_=in_, identity=ident), reads, writes)

        def ACT(out, in_, func, reads, writes, **kw):
            return S.issue("act", lambda e: e.activation(out=out, in_=in_, func=func, **kw), reads, writes)

        PS = [st.enter_context(nc.psum_tensor(f"ps{i}", [128, 512], F32)) for i in range(8)]
        PK = [f"ps{i}" for i in range(8)]

        NA32 = 51300
        arena = sb("arena", [128, NA32])
        aptr = {1: 0, 2: 0}

        def carve(which, shape, dt=F32):
            n = 1
            for q in shape[1:]:
                n *= q
            esz = 4 if dt in (F32, U32) else 2
            n32 = (n * esz + 3) // 4
            off = aptr[which]
            aptr[which] = off + n32
            assert aptr[which] <= NA32, (which, aptr[which])
            v = arena[:, off:off + n32]
            if dt != F32:
                v = v.bitcast(dt)[:, 0:n]
            if len(shape) == 3:
                v = v.rearrange("p (a b) -> p a b", a=shape[1])
            elif len(shape) == 4:
                v = v.rearrange("p (a b c) -> p a b c", a=shape[1], b=shape[2])
            return v

        pers = sb("pers", [128, 1900])
        pptr = [0]

        def PS_(name, shape, dt=F32):
            n = 1
            for q in shape[1:]:
                n *= q
            esz = 4 if dt in (F32, U32) else 2
            n32 = (n * esz + 3) // 4
            off = pptr[0]
            pptr[0] = off + n32
            assert pptr[0] <= 1900
            v = pers[:, off:off + n32]
            if dt != F32:
                v = v.bitcast(dt)[:, 0:n]
            return T(v, name)

        class T:
            def __init__(self, v, name):
                self.v = v
                self.name = name
            def __getitem__(self, k):
                return self.v[k]

        def A1(name, shape, dt=F32):
            return T(carve(1, shape, dt), name)

        def A2(name, shape, dt=F32):
            return T(carve(2, shape, dt), name)

        cst = PS_("cst_s", [128, 400])
        vecs = PS_("vecs_s", [128, NV])
        DMA("sp", cst[:, 0:256], cst_d[:, 0:256], [], ["cst"])
        DMA("sp", cst[:, 256:400], cst_d[:, 768:912], [], ["cst"])
        DMA("sp", vecs[:], vecs_d, [], ["vecs"])
        ident = cst[:, 0:128]
        iota_f = cst[:, 128:256]
        ones = cst[:, 256:384]
        iota16 = cst[:, 384:400]
        identb = PS_("identb", [128, 128], BF16)
        iotab = PS_("iotab", [128, 128], BF16)
        I("dve", "tensor_copy", ["cst"], ["identb"], out=identb[:], in_=ident)
        I("dve", "tensor_copy", ["cst"], ["iotab"], out=iotab[:], in_=iota_f)
        n1g, n2g, fng = vecs[:, 0:8], vecs[:, 8:16], vecs[:, 16:24]
        adab = vecs[:, 24:72]

        aptr[3] = 23400
        adw = [carve(3, [128, 8, 768]) for i in range(2)]
        ada_v = ada_w.rearrange("(dc p) f -> p dc f", p=128)
        cTs = PS_("cTs", [128, 8])
        cond = PS_("cond", [128, 8])
        DMA("sp", cTs[:], cT, [], ["cTs"])
        ACT(cond[:], cTs[:], AF.Silu, ["cTs"], ["cond"])
        modT = PS_("modT", [128, 48])
        for pc in range(8):
            b = pc % 2
            DMA("sp", adw[b], ada_v[:, :, pc * 768:(pc + 1) * 768], [], [("adw", b)])
            for f6 in range(6):
                fc = pc * 6 + f6
                for dc in range(8):
                    MM(PS[0][:, fc:fc + 1], adw[b][:, dc, f6 * 128:(f6 + 1) * 128], cond[:, dc:dc + 1],
                       dc == 0, dc == 7, [("adw", b), "cond"], [PK[0]])
        I("dve", "tensor_tensor", [PK[0], "vecs"], ["modT"], out=modT[:], in0=PS[0][:, 0:48], in1=adab, op=ALU.add)
        sh1, sc1, g1 = modT[:, 0:8], modT[:, 8:16], modT[:, 16:24]
        sh2, sc2, g2 = modT[:, 24:32], modT[:, 32:40], modT[:, 40:48]
        gm = PS_("gm", [128, 16])
        I("dve", "tensor_scalar", ["modT"], ["gm"], out=gm[:, 0:8], in0=sc1, scalar1=1.0, scalar2=None, op0=ALU.add)
        I("dve", "tensor_scalar", ["modT"], ["gm"], out=gm[:, 8:16], in0=sc2, scalar1=1.0, scalar2=None, op0=ALU.add)
        I("dve", "tensor_tensor", ["gm", "vecs"], ["gm"], out=gm[:, 0:8], in0=gm[:, 0:8], in1=n1g, op=ALU.mult)
        I("dve", "tensor_tensor", ["gm", "vecs"], ["gm"], out=gm[:, 8:16], in0=gm[:, 8:16], in1=n2g, op=ALU.mult)
        gm1, gm2 = gm[:, 0:8], gm[:, 8:16]

        winb = A1("winb", [128, 8, 2568], BF16)
        woutb = A1("woutb", [128, 8, 1024], BF16)
        dg = A1("dg", [128, 124, 128], BF16)
        cst1 = A1("cst1", [128, 512])
        hp = A1("hp", [128, 536])
        assert aptr[1] <= 23400
        DMA("sp", cst1[:], cst_d[:, 256:768], [], ["cst"])
        DMA("sp", hp[:], hp_d, [], ["hp"])
        tri = cst1[:, 0:128]
        blk = cst1[:, 128:256]
        chsel = [cst1[:, 256:384], cst1[:, 384:512]]
        stg_big = [carve(3, [128, 2568]) for i in range(2)]
        w_in_v = w_in.rearrange("(dc p) n -> p dc n", p=128)
        w_out_v = w_out.rearrange("(dc p) n -> p dc n", p=128)
        ci = 0
        for (src_v, dst, n) in ((w_in_v, winb, 2568), (w_out_v, woutb, 1024)):
            for dc in range(8):
                b = ci % 2
                DMA("act", stg_big[b][:, 0:n], src_v[:, dc, :], [], [("stg", b)])
                if ci % 2 == 0:
                    I("dve", "tensor_copy", [("stg", b)], [dst.name], out=dst[:, dc, :], in_=stg_big[b][:, 0:n])
                else:
                    ACT(dst[:, dc, :], stg_big[b][:, 0:n], AF.Copy, [("stg", b)], [dst.name])
                ci += 1
        for j in range(124):
            eng = "dve"
            I(eng, "tensor_scalar", ["cst", "vecs"], ["dg"], out=dg[:, j, :], in0=ident,
              scalar1=vecs[:, 116 + j:117 + j], scalar2=None, op0=ALU.mult)

        UT2 = UT.rearrange("a b c -> (a b) c").rearrange("(r q) c -> r (q c)", q=8)
        UTb2 = UTb.rearrange("a b c -> (a b) c").rearrange("(r q) c -> r (q c)", q=8)
        VR2 = VR.rearrange("a b c -> (a b) c").rearrange("(r q) c -> r (q c)", q=8)
        VRb2 = VRb.rearrange("a b c -> (a b) c").rearrange("(r q) c -> r (q c)", q=8)
        DMA("pool", WQb.rearrange("a b c -> (a b) c"), w_q.rearrange("a b c -> (a b) c"), [], ["WQb"])
        cast_jobs = []
        for i in range(16):
            cast_jobs.append((UTb2[i * 128:(i + 1) * 128, :], UT2[i * 128:(i + 1) * 128, :], ("UTb", i)))
            cast_jobs.append((VRb2[i * 128:(i + 1) * 128, :], VR2[i * 128:(i + 1) * 128, :], ("VRb", i)))

        aneg = PS_("aneg", [128, 8])
        ACT(aneg[:], hp[:, 8:16], AF.Exp, ["hp"], ["aneg"])
        I("dve", "tensor_scalar", ["aneg"], ["aneg"], out=aneg[:], in0=aneg[:], scalar1=-1.0, scalar2=None, op0=ALU.mult)
        dskip = hp[:, 24:536]

        S.barrier()
        tmpf_t = PS_("tmpf_t", [128, 512])
        tmpf = None
        rstd = PS_("rstd", [128, 512])
        rstd_full = rstd
        tmpf = [T(tmpf_t[:, 0:NB1], "tmpf"), T(tmpf_t[:, NB1:2 * NB1], "tmpf")]
        rstd = T(rstd_full[:, 0:NB1], "rstd")
        acc = tmpf
        xt = [A1(f"xt{i}", [128, 8, NB1]) for i in range(3)]
        hns = [A1(f"hn{i}", [128, 8, NB1], BF16) for i in range(2)]
        zss = [A1(f"zs{i}", [128, 4, NB1]) for i in range(2)]
        xr = A1("xr", [128, 8, 3 + NB1])
        xbcss = [A1(f"xbcs{i}", [128, 8, NB1], BF16) for i in range(2)]
        uin = A1("uin", [128, 4, 30 + NB1], BF16)
        ucv = A1("ucv", [128, 4, NB1])
        ycats = [A1(f"ycat{i}", [128, 8, NB1], BF16) for i in range(2)]
        yg = A1("yg", [128, 4, NB1])
        tmpfB = [A1(f"tmpfB{i}", [128, NB1]) for i in range(2)]
        rstdB = rstd_full[:, NB1:2 * NB1]
        mean = A1("mean", [128, NB1])
        Hf = A1("Hf", [128, 512])
        Hb = [A1(f"Hb{i}", [128, 512], BF16) for i in range(2)]
        sm = A1("sm", [128, 96])
        acs_sb = A1("acs_sb", [128, 8])
        decb = A1("decb", [128, 16])
        arg = A1("arg", [128, 8, 128])
        ex = A1("ex", [128, 8, 128])
        gmk = A1("gmk", [128, 2, 128])
        scb = A1("scb", [128, 8, 128], BF16)
        tok = A1("tok", [128, 768], BF16)
        xw = A1("xw", [128, 512], BF16)
        xd = A1("xd", [128, 512], BF16)
        ytmp = A1("ytmp", [128, 512])

        I("dve", "memset", [], ["xr"], xr[:], 0.0)
        I("dve", "memset", [], ["uin"], uin[:], 0.0)
        I("dve", "memset", [], ["Hf"], Hf[:], 0.0)
        I("dve", "memset", [], ["Hb0"], Hb[0][:], 0.0)

        xT_v = xT.rearrange("(dc p) t -> p dc t", p=128)
        h1_v = h1s.rearrange("(dc p) t -> p dc t", p=128)
        out_v = outT.rearrange("(dc p) t -> p dc t", p=128)

        def rstd_from(ps_ap, n, scale, reads_extra=(), dst=None, key="rstd"):
            d_ = rstd[:, 0:n] if dst is None else dst
            I("dve", "tensor_scalar", list(reads_extra), [key], out=d_, in0=ps_ap, scalar1=scale, scalar2=EPS,
              op0=ALU.mult, op1=ALU.add)
            ACT(d_, d_, AF.Ln, [key], [key])
            ACT(d_, d_, AF.Exp, [key], [key], scale=-0.5)

        DMA("sp", xt[0][:], xT_v[:, :, 0:NB1], [], ["xt0"])
        ncj = (32 + nblk1 - 1) // nblk1

        def stageA(blkI):
            pr = blkI % 2
            X = xt[blkI % 3]
            XK = f"xt{blkI % 3}"
            t0 = blkI * NB1
            hn = hns[pr]; HNK = f"hn{pr}"
            zs = zss[pr]; ZK = f"zs{pr}"
            xbcs = xbcss[pr]
            ycat = ycats[pr]
            if blkI + 1 < nblk1:
                DMA("sp", xt[(blkI + 1) % 3][:], xT_v[:, :, t0 + NB1:t0 + 2 * NB1], [], [f"xt{(blkI + 1) % 3}"])
            for cj in cast_jobs[blkI * ncj:(blkI + 1) * ncj]:
                DMA("pool", cj[0], cj[1], [], [cj[2]])
            yield
            for dc in range(8):
                tb = dc % 2
                ACT(tmpf[tb][:], X[:, dc, :], AF.Square, [XK], [("tmpf", tb)])
                MM(PS[0][:, 0:NB1], ones, tmpf[tb][:], dc == 0, dc == 7, [("tmpf", tb), "cst"], [PK[0]])
            rstd_from(PS[0][:, 0:NB1], NB1, 1.0 / D, [PK[0]])
            yield
            for dc in range(8):
                tb = dc % 2
                I("dve", "scalar_tensor_tensor", [XK, "gm", "rstd"], [("tmpf", tb)], out=tmpf[tb][:], in0=X[:, dc, :],
                  scalar=gm1[:, dc:dc + 1], in1=rstd[:, :], op0=ALU.mult, op1=ALU.mult)
                ACT(hn[:, dc, :], tmpf[tb][:], AF.Identity, [("tmpf", tb), "modT"], [HNK], bias=sh1[:, dc:dc + 1])
                if dc == 3:
                    yield
            yield

            def proj(col0, ncols, bank):
                for dc in range(8):
                    MM(PS[bank][0:ncols, 0:NB1], winb[:, dc, col0:col0 + ncols], hn[:, dc, :], dc == 0, dc == 7,
                       ["winb", HNK], [PK[bank]])

            for j in range(4):
                bank = j % 2
                proj(j * 128, 128, bank)
                ACT(zs[:, j, :], PS[bank][:, 0:NB1], AF.Silu, [PK[bank]], [ZK])
                if j % 2 == 1:
                    yield
            for j in range(8):
                bank = j % 2
                if blkI > 0:
                    I("dve", "tensor_copy", [("xr", j)], [("xr", j)], out=xr[:, j, 0:3], in_=xr[:, j, NB1:NB1 + 3])
                proj(512 + j * 128, 128, bank)
                ACT(xr[:, j, 3:3 + NB1], PS[bank][:, 0:NB1], AF.Copy, [PK[bank]], [("xr", j)])
                a = acc[j % 2]
                ak = ("tmpf", j % 2)
                I("dve", "tensor_scalar", [("xr", j), "vecs"], [ak], out=a[:], in0=xr[:, j, 3:3 + NB1],
                  scalar1=vecs[:, 72 + 3 * 8 + j:72 + 3 * 8 + j + 1], scalar2=None, op0=ALU.mult)
                for k in range(3):
                    I("dve", "scalar_tensor_tensor", [("xr", j), "vecs", ak], [ak], out=a[:], in0=xr[:, j, k:k + NB1],
                      scalar=vecs[:, 72 + k * 8 + j:72 + k * 8 + j + 1], in1=a[:], op0=ALU.mult, op1=ALU.add)
                ACT(xbcs[:, j, :], a[:], AF.Silu, [ak, "vecs"], [("xbcs", pr, j)], bias=vecs[:, 104 + j:105 + j])
                yield
            for j in range(4):
                if blkI > 0:
                    I("dve", "tensor_copy", [("uin", j)], [("uin", j)], out=uin[:, j, 0:30], in_=uin[:, j, NB1:NB1 + 30])
                proj(2056 + j * 128, 128, 0)
                proj(1544 + j * 128, 128, 1)
                tb = j % 2
                ACT(tmpf[tb][:], PS[0][:, 0:NB1], AF.Sigmoid, [PK[0]], [("tmpf", tb)])
                I("dve", "tensor_tensor", [PK[1], ("tmpf", tb)], [("uin", j)], out=uin[:, j, 30:30 + NB1], in0=PS[1][:, 0:NB1],
                  in1=tmpf[tb][:], op=ALU.mult)
                yield
                for k in range(31):
                    MM(PS[0][:, 0:NB1], dg[:, j * 31 + k, :], uin[:, j, k:k + NB1], k == 0, k == 30, ["dg", ("uin", j)], [PK[0]])
                    if k == 15:
                        pass
                ACT(ucv[:, j, :], PS[0][:, 0:NB1], AF.Identity, [PK[0], "vecs"], [("ucv", j)], bias=vecs[:, 240 + j:241 + j])
                yield
            for j in range(4):
                MM(PS[0][:, 0:NB1], ones, ucv[:, j, :], j == 0, j == 3, [("ucv", j), "cst"], [PK[0]])
            for j in range(4):
                tb = j % 2
                ACT(tmpf[tb][:], ucv[:, j, :], AF.Square, [("ucv", j)], [("tmpf", tb)])
                MM(PS[1][:, 0:NB1], ones, tmpf[tb][:], j == 0, j == 3, [("tmpf", tb), "cst"], [PK[1]])
            I("dve", "tensor_scalar", [PK[0]], ["mean"], out=mean[:], in0=PS[0][:, 0:NB1], scalar1=1.0 / 512, scalar2=None, op0=ALU.mult)
            I("dve", "tensor_tensor", ["mean"], [("tmpf", 0)], out=tmpf[0][:], in0=mean[:], in1=mean[:], op=ALU.mult)
            I("dve", "scalar_tensor_tensor", [PK[1], ("tmpf", 0)], [("tmpf", 1)], out=tmpf[1][:], in0=PS[1][:, 0:NB1], scalar=1.0 / 512,
              in1=tmpf[0][:], op0=ALU.mult, op1=ALU.subtract)
            rstd_from(tmpf[1][:], NB1, 1.0, [("tmpf", 1)])
            yield
            for j in range(4):
                tb = j % 2
                I("dve", "tensor_tensor", [("ucv", j), "mean"], [("tmpf", tb)], out=tmpf[tb][:], in0=ucv[:, j, :], in1=mean[:], op=ALU.subtract)
                I("dve", "tensor_tensor", [("tmpf", tb), "rstd"], [("tmpf", tb)], out=tmpf[tb][:], in0=tmpf[tb][:], in1=rstd[:, :], op=ALU.mult)
                ACT(ycat[:, 4 + j, :], tmpf[tb][:], AF.Silu, [("tmpf", tb), "vecs"], [("ycat", pr, 4 + j)],
                    scale=vecs[:, 244 + j:245 + j], bias=vecs[:, 248 + j:249 + j])
                if j == 1:
                    yield
            yield

        def stageB(blkI):
            pr = blkI % 2
            X = xt[blkI % 3]
            XK = f"xt{blkI % 3}"
            t0 = blkI * NB1
            hn = hns[pr]; HNK = f"hn{pr}"
            zs = zss[pr]; ZK = f"zs{pr}"
            xbcs = xbcss[pr]
            ycat = ycats[pr]
            XB = lambda j: ("xbcs", pr, j)
            YGK = ["yg"]
            for sbI in range(NB1 // 128):
                T0 = sbI * 128
                TS = slice(T0, T0 + 128)
                for dc in range(8):
                    MM(PS[6][:, 0:8], hn[:, dc, TS], winb[:, dc, 1536:1544], dc == 0, dc == 7, [HNK, "winb"], [PK[6]])
                xb_ = sm[:, 0:8]; ax = sm[:, 8:16]; ee = sm[:, 16:24]; dtv = sm[:, 24:32]; dA = sm[:, 32:40]
                eacs = sm[:, 40:48]; toend = sm[:, 48:56]; tt_ = sm[:, 56:64]
                I("dve", "tensor_tensor", [PK[6], "hp"], ["sm"], out=xb_, in0=PS[6][:, 0:8], in1=hp[:, 0:8], op=ALU.add)
                ACT(ee, xb_, AF.Exp, ["sm"], ["sm"])
                ACT(dtv, ee, AF.Ln, ["sm", "cst"], ["sm"], bias=ones[:, 0:1])
                yield
                I("dve", "tensor_tensor", ["sm", "aneg"], ["sm"], out=dA, in0=dtv, in1=aneg[:], op=ALU.mult)
                MM(PS[6][:, 8:16], tri, dA, True, True, ["sm", "cst"], [PK[6]])
                MM(PS[6][:, 16:24], blk, dA, True, True, ["sm", "cst"], [PK[6]])
                MM(PS[6][:, 24:32], chsel[0], dA, True, True, ["sm", "cst"], [PK[6]])
                MM(PS[6][:, 32:40], chsel[1], dA, True, True, ["sm", "cst"], [PK[6]])
                for h in range(8):
                    bank = 4 + h // 4
                    MM(PS[bank][:, (h % 4) * 128:(h % 4 + 1) * 128], dA[:, h:h + 1].to_broadcast([128, 128]), tri, True, True,
                       ["sm", "cst"], [PK[bank]])
                yield
                I("dve", "tensor_copy", [PK[6]], ["acs_sb"], out=acs_sb[:], in_=PS[6][:, 8:16])
                ACT(eacs, PS[6][:, 8:16], AF.Exp, [PK[6]], ["sm2"])
                I("dve", "tensor_tensor", [PK[6], "acs_sb"], ["sm2"], out=tt_, in0=PS[6][:, 16:24], in1=acs_sb[:], op=ALU.subtract)
                ACT(tt_, tt_, AF.Exp, ["sm2"], ["sm2"])
                I("dve", "tensor_tensor", ["sm2", "sm"], ["sm2"], out=toend, in0=tt_, in1=dtv, op=ALU.mult)
                ACT(decb[:], PS[6][:, 24:40], AF.Exp, [PK[6]], ["decb"])
                yield
                for hh in range(2):
                    bank = 4 + hh
                    I("dve", "tensor_tensor", [PK[bank], "acs_sb"], [("arg", hh)], out=arg[:, hh * 4:(hh + 1) * 4, :],
                      in0=PS[bank][:, :].rearrange("p (h t) -> p h t", h=4),
                      in1=acs_sb[:, hh * 4:(hh + 1) * 4].unsqueeze(2).to_broadcast([128, 4, 128]), op=ALU.subtract)
                    I("dve", "tensor_scalar", [("arg", hh)], [("arg", hh)], out=arg[:, hh * 4:(hh + 1) * 4, :],
                      in0=arg[:, hh * 4:(hh + 1) * 4, :], scalar1=0.0, scalar2=None, op0=ALU.min)
                    ACT(ex[:, hh * 4:(hh + 1) * 4, :], arg[:, hh * 4:(hh + 1) * 4, :], AF.Exp, [("arg", hh)], [("ex", hh)])
                for g in range(2):
                    MM(PS[6][:, 128 + g * 128:256 + g * 128], xbcs[:, 4 + g, TS], xbcs[:, 6 + g, TS], True, True,
                       [XB(4 + g), XB(6 + g)], [PK[6]])
                yield
                I("dve", "tensor_tensor", [PK[6], "cst"], ["gmk"], out=gmk[:], in0=PS[6][:, 128:384].rearrange("p (g t) -> p g t", g=2),
                  in1=tri.unsqueeze(1).to_broadcast([128, 2, 128]), op=ALU.mult)
                for g in range(2):
                    I("dve", "tensor_tensor", [("ex", g), "gmk"], [("ex", g)], out=ex[:, g * 4:(g + 1) * 4, :], in0=ex[:, g * 4:(g + 1) * 4, :],
                      in1=gmk[:, g:g + 1, :].to_broadcast([128, 4, 128]), op=ALU.mult)
                I("dve", "tensor_tensor", [("ex", 0), ("ex", 1), "sm"], ["scb"], out=scb[:], in0=ex[:],
                  in1=dtv.unsqueeze(2).to_broadcast([128, 8, 128]), op=ALU.mult)
                trp = PS[3][:, :].bitcast(BF16)
                for j in range(6):
                    TR(trp[:, j * 128:(j + 1) * 128], xbcs[:, j, TS], identb[:], [XB(j), "identb"], [PK[3]])
                ACT(tok[:], trp[:, 0:768], AF.Copy, [PK[3]], ["tok"])
                yield
                I("dve", "tensor_tensor", ["tok", "sm2"], ["xw"], out=xw[:].rearrange("p (h q) -> p h q", h=8),
                  in0=tok[:, 0:512].rearrange("p (h q) -> p h q", h=8), in1=toend.unsqueeze(2).to_broadcast([128, 8, 64]), op=ALU.mult)
                I("pool", "tensor_tensor", ["tok", "hp"], ["xd"], out=xd[:], in0=tok[:, 0:512], in1=dskip, op=ALU.mult)
                for h in range(8):
                    hs = slice(h * 64, (h + 1) * 64)
                    MM(PS[2][:, hs], scb[:, h, :], tok[:, hs], True, False, ["scb", "tok"], [PK[2]])
                    MM(PS[2][:, hs], identb[:], xd[:, hs], False, True, ["identb", "xd"], [PK[2]])
                yield
                for c in range(2):
                    R = slice(c * 64, (c + 1) * 64)
                    hb_in = Hb[c]
                    hb_out = Hb[(c + 1) % 2]
                    for g in range(2):
                        gs = slice(g * 256, (g + 1) * 256)
                        MM(PS[7][R, gs], xbcs[:, 6 + g, T0 + c * 64:T0 + (c + 1) * 64], hb_in[:, gs], True, True,
                           [XB(6 + g), f"Hb{c}"], [PK[7]])
                        MM(PS[3][:, gs], tok[R, 512 + g * 128:512 + (g + 1) * 128], xw[R, gs], True, True, ["tok", "xw"], [PK[3]])
                    I("dve", "tensor_tensor", ["Hf", "decb"], ["Hf"], out=Hf[:].rearrange("p (h q) -> p h q", h=8),
                      in0=Hf[:].rearrange("p (h q) -> p h q", h=8),
                      in1=decb[:, c * 8:(c + 1) * 8].unsqueeze(2).to_broadcast([128, 8, 64]), op=ALU.mult)
                    I("dve", "tensor_tensor", ["Hf", PK[3]], ["Hf"], out=Hf[:], in0=Hf[:], in1=PS[3][:, :], op=ALU.add)
                    ACT(hb_out[:], Hf[:], AF.Copy, ["Hf"], [f"Hb{(c + 1) % 2}"])
                    I("dve", "tensor_tensor", [PK[7], "sm2"], ["ytmp"], out=ytmp[R, :].rearrange("p (h q) -> p h q", h=8),
                      in0=PS[7][R, :].rearrange("p (h q) -> p h q", h=8), in1=eacs[R, :].unsqueeze(2).to_broadcast([64, 8, 64]), op=ALU.mult)
                    yield
                I("dve", "tensor_tensor", ["ytmp", PK[2]], ["ytmp"], out=ytmp[:], in0=ytmp[:], in1=PS[2][:, :], op=ALU.add)
                for j in range(4):
                    TR(PS[2][:, j * 128:(j + 1) * 128], ytmp[:, j * 128:(j + 1) * 128], ident, ["ytmp", "cst"], [PK[2]])
                I("dve", "tensor_tensor", [PK[2], ZK], YGK, out=yg[:, :, TS], in0=PS[2][:, :].rearrange("p (j t) -> p j t", j=4),
                  in1=zs[:, :, TS], op=ALU.mult)
                yield
            for j in range(4):
                tb = j % 2
                ACT(tmpfB[tb][:], yg[:, j, :], AF.Square, YGK, [("tmpfB", tb)])
                MM(PS[7][:, 0:NB1], ones, tmpfB[tb][:], j == 0, j == 3, [("tmpfB", tb), "cst"], [PK[7]])
            rstd_from(PS[7][:, 0:NB1], NB1, 1.0 / 512, [PK[7]], dst=rstdB, key="rstdB")
            yield
            for j in range(4):
                I("dve", "scalar_tensor_tensor", YGK + ["vecs", "rstdB"], [("ycat", pr, j)], out=ycat[:, j, :], in0=yg[:, j, :],
                  scalar=vecs[:, 112 + j:113 + j], in1=rstdB, op0=ALU.mult, op1=ALU.mult)
            yield
            ROTB = (2, 3, 4, 5)
            for dcn in range(8):
                bank = ROTB[dcn % 4]
                for kc in range(8):
                    MM(PS[bank][:, 0:NB1], woutb[:, kc, dcn * 128:(dcn + 1) * 128], ycat[:, kc, :], kc == 0, kc == 7,
                       ["woutb"] + [("ycat", pr, q) for q in range(8)], [PK[bank]])
                I("dve", "scalar_tensor_tensor", [PK[bank], "modT", XK], [XK], out=X[:, dcn, :], in0=PS[bank][:, 0:NB1],
                  scalar=g1[:, dcn:dcn + 1], in1=X[:, dcn, :], op0=ALU.mult, op1=ALU.add)
                if dcn % 2 == 1:
                    yield
            DMA("sp", h1_v[:, :, t0:t0 + NB1], X[:], [XK], ["h1s"])
            yield

        for _ in stageA(0):
            pass
        for blkI in range(nblk1):
            gB = stageB(blkI)
            gA = stageA(blkI + 1) if blkI + 1 < nblk1 else None
            doneA = gA is None
            doneB = False
            while not (doneA and doneB):
                if not doneB:
                    try:
                        next(gB)
                    except StopIteration:
                        doneB = True
                if not doneA:
                    try:
                        next(gA)
                    except StopIteration:
                        doneA = True

        S.barrier()
        rstd = rstd_full
        TB = 256
        nblk2 = S_TOK // TB
        NT = TB // 128
        NG = TB // 4
        ABufs = [carve(2, [128, 128, TB], BF16) for _ in range(2)]
        hn2s = [A2(f"hn2{i}", [128, 8, TB], BF16) for i in range(2)]
        keysb = A2("keysb", [128, 2048], BF16)
        DMA("pool", keysb[:], keysT, [], ["keysb"])
        kgTs = [A2(f"kgT{i}", [128, 3, TB]) for i in range(2)]
        NUB, NVB = 4, 6
        ub = [A2(f"ub{i}", [128, 8, 128], BF16) for i in range(NUB)]
        vb = [A2(f"vb{i}", [128, 1, 1024], BF16) for i in range(NVB)]
        P0 = [T(vb[i][:, 0, 0:512].rearrange("p (a b) -> p a b", a=4), "P0") for i in range(2)]
        P1 = [T(vb[i][:, 0, 512:1024].rearrange("p (a b) -> p a b", a=4), "P1") for i in range(2)]
        aab = [A2(f"aab{i}", [128, 128], BF16) for i in range(2)]
        hb2 = carve(2, [128, 8, TB])
        rs0 = aptr[2]
        sc = A2("sc", [128, 16, 128])
        hbv = arena[:, rs0:rs0 + 2048].rearrange("p (a b) -> p a b", a=8)
        cand = A2("cand", [128, 4, 256])
        qT = A2("qT", [128, 16, 128], BF16)
        m16 = A2("m16", [128, 16, 16])
        i16 = A2("i16", [128, 16, 16], U32)
        i16f = A2("i16f", [128, 16, 16])
        b16 = A2("b16", [128, 8, 16])
        p16 = A2("p16", [128, 8, 16], U32)
        pab = A2("pab", [128, 2, 128], U32)
        pabf = A2("pabf", [128, 2, 128])
        kgf2 = [A2(f"kgf{i}", [128, 3, 128]) for i in range(2)]
        zz = A2("zz", [128, 16])
        eq = T(sc[:].rearrange("p a k -> p (a k)").rearrange("p (x y) -> p x y", y=16), "eq")
        tmpA = tmpf_t[:, 0:256]
        tmpB = tmpf_t[:, 256:512]
        SCK = [("sc", q) for q in range(16)]
        CANDK = [("cand", h) for h in range(4)]
        M16K = [("m16", q, v) for q in range(16) for v in range(2)]
        I16K = [("i16", q, v) for q in range(16) for v in range(2)]
        B16K = [("b16", h, v) for h in range(8) for v in range(2)]
        P16K = [("p16", h, v) for h in range(8) for v in range(2)]
        HBK = SCK
        ABKs = [[("AB", i, q) for q in range(8)] for i in range(2)]
        UTb_v = UTb.rearrange("k p (dc q) -> p k dc q", dc=8)
        VRb_v = VRb.rearrange("k q d -> q k d")
        CERF = 1.0

        def stream(n, nbuf, load_fn, use_fn, after=None):
            for i in range(min(nbuf - 1, n)):
                load_fn(i, i % nbuf)
            for i in range(n):
                if i + nbuf - 1 < n:
                    load_fn(i + nbuf - 1, (i + nbuf - 1) % nbuf)
                use_fn(i, i % nbuf)
                if after is not None:
                    after(i)

        def routing_steps(bb):
            t0 = bb * TB
            hn2 = hn2s[bb % 2]
            hk = f"hn2{bb % 2}"
            kgT = kgTs[bb % 2]
            kk_ = f"kgT{bb % 2}"
            steps = []
            steps.append(lambda: DMA("sp", hbv, h1_v[:, :, t0:t0 + TB], ["h1s"], HBK))

            def st_stats(lo, hi):
                def f():
                    for dc in range(lo, hi):
                        ACT(tmpA, hbv[:, dc, :], AF.Square, HBK, [("tmpf", 0)])
                        MM(PS[4][:, 0:TB], ones, tmpA, dc == 0, dc == 7, [("tmpf", 0), "cst"], [PK[4]])
                return f
            steps.append(st_stats(0, 4))
            steps.append(st_stats(4, 8))
            steps.append(lambda: rstd_from(PS[4][:, 0:TB], TB, 1.0 / D, [PK[4]]))

            def st_hn(lo, hi):
                def f():
                    for dc in range(lo, hi):
                        I("dve", "scalar_tensor_tensor", HBK + ["gm", "rstd"], [("tmpf", 0)], out=tmpA, in0=hbv[:, dc, :],
                          scalar=gm2[:, dc:dc + 1], in1=rstd[:, 0:TB], op0=ALU.mult, op1=ALU.mult)
                        ACT(hn2[:, dc, :], tmpA, AF.Identity, [("tmpf", 0), "modT"], [hk], bias=sh2[:, dc:dc + 1])
                return f
            steps.append(st_hn(0, 4))
            steps.append(st_hn(4, 8))

            def q_steps(tt):
                TS = slice(tt * 128, (tt + 1) * 128)
                out = []

                NQ = NUB - 1

                def ldq(i):
                    bf = i % NQ
                    DMA("sp", ub[bf][:], WQb[i].rearrange("p (dc c) -> p dc c", dc=8), ["WQb"], [f"ub{bf}"])

                def mk(i):
                    def f():
                        if i == 0:
                            for j in range(NQ - 1):
                                ldq(j)
                        if i + NQ - 1 < 16:
                            ldq(i + NQ - 1)
                        bf = i % NQ
                        bank = 4 + (i // 4) % 2
                        for dc in range(8):
                            MM(PS[bank][:, (i % 4) * 128:(i % 4 + 1) * 128], ub[bf][:, dc, :], hn2[:, dc, TS], dc == 0, dc == 7,
                               [f"ub{bf}", hk], [PK[bank]])
                        if i % 4 == 3:
                            g4 = i // 4
                            ACT(qT[:, g4 * 4:(g4 + 1) * 4, :], PS[bank][:, :].rearrange("p (a t) -> p a t", a=4), AF.Copy, [PK[bank]], ["qT"])
                    return f
                for i in range(16):
                    out.append(mk(i))
                return out

            def scores_step(tt):
                def f():
                    for half in range(2):
                        for q8 in range(8):
                            qc = half * 8 + q8
                            bank = 6 + q8 // 4
                            MM(PS[bank][:, (q8 % 4) * 128:(q8 % 4 + 1) * 128], qT[:, qc, :], keysb[:, qc * 128:(qc + 1) * 128], True, True,
                               ["qT", "keysb"], [PK[bank]])
                        for bb_ in range(2):
                            q0 = half * 8 + bb_ * 4
                            ACT(sc[:, q0:q0 + 4, :], PS[6 + bb_][:, :].rearrange("p (a k) -> p a k", a=4), AF.Copy, [PK[6 + bb_]],
                                [("sc", q0 + q) for q in range(4)])
                return f

            def chain_step(tt):
                kgf = kgf2[tt % 2]
                kp = tt % 2

                def f():
                    for qc in range(16):
                        I("dve", "max", [("sc", qc)], [("m16", qc, 0)], out=m16[:, qc, 0:8], in_=sc[:, qc, :])
                    for qc in range(16):
                        I("dve", "max_index", [("sc", qc), ("m16", qc, 0)], [("i16", qc, 0)], out=i16[:, qc, 0:8], in_max=m16[:, qc, 0:8], in_values=sc[:, qc, :])
                    for qc in range(16):
                        I("dve", "match_replace", [("sc", qc), ("m16", qc, 0)], [("sc", qc)], out=sc[:, qc, :], in_to_replace=m16[:, qc, 0:8], in_values=sc[:, qc, :], imm_value=NEG)
                    for qc in range(16):
                        I("dve", "max", [("sc", qc)], [("m16", qc, 1)], out=m16[:, qc, 8:16], in_=sc[:, qc, :])
                    for qc in range(16):
                        I("dve", "max_index", [("sc", qc), ("m16", qc, 1)], [("i16", qc, 1)], out=i16[:, qc, 8:16], in_max=m16[:, qc, 8:16], in_values=sc[:, qc, :])
                    I("dve", "tensor_copy", I16K, ["i16f"], out=i16f[:], in_=i16[:])
                    m16v = m16[:].rearrange("p (h i) a -> p h i a", i=2)
                    i16v = i16f[:].rearrange("p (h i) a -> p h i a", i=2)
                    for hh in range(2):
                        H4 = slice(hh * 4, (hh + 1) * 4)
                        I("dve", "tensor_tensor", M16K, CANDK, out=cand[:].rearrange("p h (a b) -> p h a b", a=16),
                          in0=m16v[:, H4, 0, :].unsqueeze(3).to_broadcast([128, 4, 16, 16]),
                          in1=m16v[:, H4, 1, :].unsqueeze(2).to_broadcast([128, 4, 16, 16]), op=ALU.add)
                        for h4 in range(4):
                            h = hh * 4 + h4
                            I("dve", "max", [("cand", h4)], [("b16", h, 0)], out=b16[:, h, 0:8], in_=cand[:, h4, :])
                        for h4 in range(4):
                            h = hh * 4 + h4
                            I("dve", "max_index", [("cand", h4), ("b16", h, 0)], [("p16", h, 0)], out=p16[:, h, 0:8], in_max=b16[:, h, 0:8], in_values=cand[:, h4, :])
                        for h4 in range(4):
                            h = hh * 4 + h4
                            I("dve", "match_replace", [("cand", h4), ("b16", h, 0)], [("cand", h4)], out=cand[:, h4, :], in_to_replace=b16[:, h, 0:8], in_values=cand[:, h4, :], imm_value=NEG)
                        for h4 in range(4):
                            h = hh * 4 + h4
                            I("dve", "max", [("cand", h4)], [("b16", h, 1)], out=b16[:, h, 8:16], in_=cand[:, h4, :])
                        for h4 in range(4):
                            h = hh * 4 + h4
                            I("dve", "max_index", [("cand", h4), ("b16", h, 1)], [("p16", h, 1)], out=p16[:, h, 8:16], in_max=b16[:, h, 8:16], in_values=cand[:, h4, :])
                    p16f = p16[:].rearrange("p h j -> p (h j)")
                    I("dve", "tensor_single_scalar", P16K, ["pab"], out=pab[:, 0, :], in_=p16f, scalar=4, op=ALU.logical_shift_right)
                    I("dve", "tensor_single_scalar", P16K, ["pab"], out=pab[:, 1, :], in_=p16f, scalar=15, op=ALU.bitwise_and)
                    I("dve", "tensor_copy", ["pab"], ["pabf"], out=pabf[:], in_=pab[:])
                    for i2 in range(2):
                        I("dve", "tensor_tensor", ["pabf", "cst"], SCK, out=eq[:].rearrange("p (h j) a -> p h j a", h=8),
                          in0=iota16.unsqueeze(1).unsqueeze(1).to_broadcast([128, 8, 16, 16]),
                          in1=pabf[:, i2, :].rearrange("p (h j) -> p h j", h=8).unsqueeze(3).to_broadcast([128, 8, 16, 16]), op=ALU.is_equal)
                        I("dve", "tensor_tensor", SCK + ["i16f"], SCK, out=eq[:].rearrange("p (h j) a -> p h j a", h=8),
                          in0=eq[:].rearrange("p (h j) a -> p h j a", h=8),
                          in1=i16v[:, :, i2, :].unsqueeze(2).to_broadcast([128, 8, 16, 16]), op=ALU.mult)
                        I("dve", "tensor_reduce", SCK, [("kgf", kp, i2)], out=kgf[:, i2, :], in_=eq[:], axis=AX.X, op=ALU.add)
                    gv = kgf[:, 2, :].rearrange("p (h j) -> p h j", h=8)
                    I("dve", "tensor_tensor", B16K, [("kgf", kp, 2)], out=gv, in0=b16[:], in1=b16[:, :, 0:1].to_broadcast([128, 8, 16]), op=ALU.subtract)
                    ACT(gv, gv, AF.Exp, [("kgf", kp, 2)], [("kgf", kp, 2)])
                    I("dve", "tensor_reduce", [("kgf", kp, 2)], ["zz"], out=zz[:, 0:8], in_=gv, axis=AX.X, op=ALU.add)
                    I("dve", "tensor_scalar", ["zz"], ["zz"], out=zz[:, 0:8], in0=zz[:, 0:8], scalar1=CERF, scalar2=None, op0=ALU.mult)
                    I("dve", "reciprocal", ["zz"], ["zz"], out=zz[:, 8:16], in_=zz[:, 0:8])
                    I("dve", "tensor_tensor", [("kgf", kp, 2), "zz"], [("kgf", kp, 2)], out=gv, in0=gv, in1=zz[:, 8:16].unsqueeze(2).to_broadcast([128, 8, 16]), op=ALU.mult)
                return f

            def tr_step(tt, bank):
                TS = slice(tt * 128, (tt + 1) * 128)
                kgf = kgf2[tt % 2]
                kp = tt % 2

                def f():
                    KG = [("kgf", kp, q) for q in range(3)]
                    for q in range(3):
                        TR(PS[bank][:, q * 128:(q + 1) * 128], kgf[:, q, :], ident, KG + ["cst"], [PK[bank]])
                    ACT(kgT[:, :, TS], PS[bank][:, 0:384].rearrange("p (q t) -> p q t", q=3), AF.Copy, [PK[bank]], [kk_])
                return f

            return dict(pre=steps, q=[q_steps(tt) for tt in range(NT)], sc=[scores_step(tt) for tt in range(NT)],
                        ch=[chain_step(tt) for tt in range(NT)], tr=[tr_step(tt, 6) for tt in range(NT)], tr_spill=tr_step(NT - 1, 2))

        def run_routing_now(R):
            for f in R["pre"]:
                f()
            for tt in range(NT):
                for f in R["q"][tt]:
                    f()
                R["sc"][tt]()
                R["ch"][tt]()
                R["tr"][tt]()

        def A_phase(bb, after=None):
            AB = ABufs[bb % 2]
            hn2 = hn2s[bb % 2]
            hk = f"hn2{bb % 2}"

            ring = [(ub[i][:], f"ub{i}") for i in range(NUB)] + \
                   [(vb[i][:].rearrange("p a (dc c) -> p (a dc) c", c=128), f"vb{i}") for i in range(2, NVB)]

            def ldu(i, bf):
                DMA("sp", ring[bf][0], UTb_v[:, i, :, :], [("UTb", i // 8)], [ring[bf][1]])

            def useu(k1, bf):
                bank = (4, 5, 0, 1)[k1 % 4]
                for dc in range(8):
                    MM(PS[bank][:, 0:TB], ring[bf][0][:, dc, :], hn2[:, dc, :], dc == 0, dc == 7, [ring[bf][1], hk], [PK[bank]])
                ACT(AB[:, k1, :], PS[bank][:, 0:TB], AF.Gelu, [PK[bank]], [("AB", bb % 2, k1 // 16)])
            stream(128, len(ring), ldu, useu, after=after)

        def W_fns(bb):
            AB = ABufs[bb % 2]
            ABK = ABKs[bb % 2]
            kgT = kgTs[bb % 2]
            kk_ = f"kgT{bb % 2}"

            def w_build(tg):
                pb = tg % 2
                xk = [f"vb{pb}"] if tg < 2 else []
                for q in range(4):
                    t = tg * 4 + q
                    if q == 0:
                        ACT(aab[pb][:], iotab[:], AF.Abs, ["iotab", kk_], [("aab", pb)], scale=-1.0, bias=kgT[:, 0, t:t + 1])
                        ACT(P0[pb][:, q, :], aab[pb][:], AF.Relu, [("aab", pb), "cst"], [("P0", pb, q)] + xk, scale=-1.0, bias=ones[:, 0:1])
                    else:
                        I("dve", "tensor_scalar", ["iotab", kk_], [("P0", pb, q)] + xk, out=P0[pb][:, q, :], in0=iotab[:], scalar1=kgT[:, 0, t:t + 1],
                          scalar2=None, op0=ALU.is_equal)
                    I("dve", "tensor_scalar", ["iotab", kk_], [("P1", pb)] + xk, out=P1[pb][:, q, :], in0=iotab[:], scalar1=kgT[:, 1, t:t + 1],
                      scalar2=kgT[:, 2, t:t + 1], op0=ALU.is_equal, op1=ALU.mult)

            def w_apply(tg):
                pb = tg % 2
                bank = 6 + pb
                for q in range(4):
                    MM(PS[bank][:, q * 128:(q + 1) * 128], P0[pb][:, q, :], P1[pb][:, q, :], True, True, [("P0", pb, q), ("P1", pb)], [PK[bank]])
                I("dve", "tensor_tensor", [PK[bank]] + ABK, ABK, out=AB[:, :, tg * 4:(tg + 1) * 4],
                  in0=PS[bank][:, :].rearrange("p (t k) -> p k t", t=4), in1=AB[:, :, tg * 4:(tg + 1) * 4], op=ALU.mult)
            return w_build, w_apply

        def V_phase(bb, after=None):
            AB = ABufs[bb % 2]
            ABK = ABKs[bb % 2]
            t0 = bb * TB
            DMA("sp", hb2, h1_v[:, :, t0:t0 + TB], ["h1s"], ["hb2"])
            ringv = [(vb[i][:, 0, :], f"vb{i}", ([("P0", i, q) for q in range(4)] + [("P1", i)]) if i < 2 else []) for i in range(NVB)] + \
                    [(ub[NUB - 1][:].rearrange("p a b -> p (a b)"), f"ub{NUB - 1}", [])]

            def ldv(i, bf):
                DMA("sp", ringv[bf][0], VRb_v[:, i, :], [("VRb", i // 8)], [ringv[bf][1]] + ringv[bf][2])

            def usev(k1, bf):
                for dq in range(8):
                    MM(PS[dq // 2][:, (dq % 2) * TB:(dq % 2 + 1) * TB], ringv[bf][0][:, dq * 128:(dq + 1) * 128], AB[:, k1, :],
                       k1 == 0, k1 == 127, [ringv[bf][1]] + ABK, [PK[dq // 2]])

            def aft(i):
                if after is not None and i % 2 == 1:
                    after(i // 2)
            stream(128, len(ringv), ldv, usev, after=aft)
            for dq in range(8):
                tk = ("tmpf", dq % 2)
                tmp = tmpA if dq % 2 == 0 else tmpB
                ACT(tmp, PS[dq // 2][:, (dq % 2) * TB:(dq % 2 + 1) * TB], AF.Copy, [PK[dq // 2], "modT"], [tk], scale=g2[:, dq:dq + 1])
                I("pool", "tensor_tensor", [tk, "hb2"], ["hb2"], out=hb2[:, dq, :], in0=hb2[:, dq, :], in1=tmp, op=ALU.add)

        def final_steps(bb):
            t0 = bb * TB

            def fin_stats():
                for dc in range(8):
                    ACT(tmpA, hb2[:, dc, :], AF.Square, ["hb2"], [("tmpf", 0)])
                    MM(PS[3][:, 0:TB], ones, tmpA, dc == 0, dc == 7, [("tmpf", 0), "cst"], [PK[3]])

            def fin_rstd():
                rstd_from(PS[3][:, 0:TB], TB, 1.0 / D, [PK[3]])

            def fin_out():
                for dc in range(8):
                    I("dve", "scalar_tensor_tensor", ["hb2", "vecs", "rstd"], ["hb2"], out=hb2[:, dc, :], in0=hb2[:, dc, :],
                      scalar=fng[:, dc:dc + 1], in1=rstd[:, 0:TB], op0=ALU.mult, op1=ALU.mult)
                DMA("sp", out_v[:, :, t0:t0 + TB], hb2, ["hb2"], ["outT"])
            return [fin_stats, fin_rstd, fin_out]

        run_routing_now(routing_steps(0))
        if nblk2 > 1:
            run_routing_now(routing_steps(1))
        A_phase(0)
        pend_final = None
        spill = None
        for b2 in range(nblk2):
            w_build, w_apply = W_fns(b2)
            sched1 = {}
            if pend_final is not None:
                sched1.setdefault(6, []).append(pend_final[0])
                sched1.setdefault(14, []).append(pend_final[1])
                sched1.setdefault(22, []).append(pend_final[2])
            if spill is not None:
                sched1.setdefault(50, []).append(spill)
            w_build(0)

            def after_a(i):
                for f in sched1.pop(i, []):
                    f()
                if i % 2 == 1:
                    g = i // 2
                    if g + 1 < NG:
                        w_build(g + 1)
                    w_apply(g)
            if b2 + 1 < nblk2:
                A_phase(b2 + 1, after=after_a)
            else:
                for i in range(128):
                    after_a(i)
            for i in sorted(sched1):
                for f in sched1[i]:
                    f()
            RN = routing_steps(b2 + 2) if b2 + 2 < nblk2 else None
            vsched = {}
            if RN is not None:
                def put(i, f):
                    vsched.setdefault(i, []).append(f)
                for j, f in enumerate(RN["pre"]):
                    put(j // 2, f)
                for j, f in enumerate(RN["q"][0]):
                    put(4 + j, f)
                put(20, RN["sc"][0])
                put(20, RN["ch"][0])
                for j, f in enumerate(RN["q"][1]):
                    put(22 + j, f)
                put(38, RN["sc"][1])
                put(38, RN["ch"][1])
                put(57, RN["tr"][0])
                spill = RN["tr_spill"]
            else:
                spill = None
            vcount = [0]

            def after_v(i):
                for f in vsched.pop(vcount[0], []):
                    f()
                vcount[0] += 1
            V_phase(b2, after=after_v)
            for i in sorted(vsched):
                for f in vsched[i]:
                    f()
            pend_final = final_steps(b2)
        for f in pend_final:
            f()
        S.wait_all("sp", ["outT", "h1s"])
        S.emit(st)
    return nc


def _consts():
    c = np.zeros((128, NC), np.float32)
    i = np.arange(128)
    c[:, 0:128] = np.eye(128, dtype=np.float32)
    c[:, 128:256] = i[None, :]
    same = (i[:, None] // 64) == (i[None, :] // 64)
    c[:, 256:384] = (same & (i[:, None] <= i[None, :])).astype(np.float32)
    c[:, 384:512] = same.astype(np.float32)
    c[:, 512:640] = (i[:, None] < 64).astype(np.float32) * np.ones((1, 128), np.float32)
    c[:, 640:768] = (i[:, None] >= 64).astype(np.float32) * np.ones((1, 128), np.float32)
    c[:, 768:896] = 1.0
    c[:, 896:912] = np.arange(16)[None, :]
    return c


def _fm(v, nch):
    return np.ascontiguousarray(np.asarray(v, np.float32).reshape(nch, 128).T)


def prep_shared(inp):
    f = lambda k: np.asarray(inp[k], np.float32)
    vecs = np.zeros((128, NV), np.float32)
    vecs[:, 0:8] = _fm(f("norm1_g")[0], 8)
    vecs[:, 8:16] = _fm(f("norm2_g")[0], 8)
    vecs[:, 16:24] = _fm(f("final_norm_g"), 8)
    vecs[:, 24:72] = _fm(f("ada_b")[0], 48)
    cw = f("ssd_conv_w")[0]
    for k in range(4):
        vecs[:, 72 + k * 8:72 + (k + 1) * 8] = _fm(cw[k], 8)
    vecs[:, 104:112] = _fm(f("ssd_conv_b")[0], 8)
    vecs[:, 112:116] = _fm(f("ssd_norm_g")[0], 4)
    dw = f("conf_dw_w")[0]
    for j in range(4):
        for k in range(31):
            vecs[:, 116 + j * 31 + k] = dw[k, j * 128:(j + 1) * 128]
    vecs[:, 240:244] = _fm(f("conf_dw_b")[0], 4)
    vecs[:, 244:248] = _fm(f("conf_ln_g")[0], 4)
    vecs[:, 248:252] = _fm(f("conf_ln_b")[0], 4)
    hp = np.zeros((128, 536), np.float32)
    hp[:, 0:8] = f("ssd_dt_bias")[0][None, :]
    hp[:, 8:16] = f("ssd_a_log")[0][None, :]
    hp[:, 24:536] = np.repeat(f("ssd_d")[0], 64)[None, :]
    keys = f("peer_sub_keys")[0]
    keysT = np.ascontiguousarray(keys.transpose(3, 0, 1, 2).reshape(128, 2048))
    U = f("peer_u")[0].reshape(128, 128, 8, 128)
    UT = np.ascontiguousarray(U.transpose(1, 3, 2, 0)).reshape(128, 128, 1024)
    V = f("peer_v")[0].reshape(128, 128, 1024)
    VR = np.ascontiguousarray(V.transpose(1, 0, 2))
    wq = f("peer_w_query")[0].reshape(8, 128, 16, 128)
    WQ = np.ascontiguousarray(wq.transpose(2, 1, 0, 3)).reshape(16, 128, 1024)
    return dict(ada_w=np.ascontiguousarray(f("ada_w")[0]), vecs=vecs, hp=hp,
                w_in=np.ascontiguousarray(f("w_in")[0]), w_out=np.ascontiguousarray(f("w_out")[0]),
                w_q=WQ, keysT=keysT, UT=UT, VR=VR, cst=_consts())


def prep_core(inp, b, s_tok):
    x = np.asarray(inp["x"], np.float32)[b, :s_tok]
    c = np.asarray(inp["c"], np.float32)[b]
    return dict(xT=np.ascontiguousarray(x.T), cT=_fm(c, 8))


def kernel(**inputs):
    B, S_TOK = 8, 4096
    nc = build(S_TOK)
    shared = prep_shared(inputs)
    in_maps = []
    for b in range(B):
        m = dict(shared)
        m.update(prep_core(inputs, b, S_TOK))
        in_maps.append(m)
    res = run_bass_kernel_spmd(nc, in_maps, core_ids=list(range(B)))
    out = np.stack([np.ascontiguousarray(np.asarray(r["outT"], np.float32).T) for r in res.results], axis=0)
    return out
```
